# Optimizing a Trainium2 kernel written in Bass

```python
import math
import jax, jax.numpy as jnp
from jax import lax
import numpy as np

D_MODEL = 1024
BATCH = 8
SEQ = 4096
DEPTH = 1

N_META = 16
D_SSM = D_MODEL // 2
D_POOL = D_MODEL - D_SSM
SSM_GROUP = 16
SSM_GROUPS = D_SSM // SSM_GROUP
SSM_STATE = 64
POOL_WINDOWS = (2, 4, 8, 16)
POOL_GROUPS = len(POOL_WINDOWS)
POOL_GROUP_DIM = D_POOL // POOL_GROUPS
D_FF = ((8 * D_MODEL // 3 + 255) // 256) * 256
STEP_MIN = 1e-3
STEP_MAX = 1e-1
EPS = 1e-6

kernel_name = "hymba_s5_multiscale_pool_hybrid"


def rmsnorm(x, g):
    xf = x.astype(jnp.float32)
    return xf * lax.rsqrt(jnp.mean(xf * xf, axis=-1, keepdims=True) + EPS) * g.astype(jnp.float32)


def _complex_affine_combine(e1, e2):
    a1r, a1i, b1r, b1i = e1
    a2r, a2i, b2r, b2i = e2
    ar = a2r * a1r - a2i * a1i
    ai = a2r * a1i + a2i * a1r
    br = a2r * b1r - a2i * b1i + b2r
    bi = a2r * b1i + a2i * b1r + b2i
    return (ar, ai, br, bi)


def s5_mixer(u, lam_re, lam_im, log_step, b_re, b_im, c_re, c_im, d, glu_w, glu_b):
    L = u.shape[1]
    lr = jnp.minimum(lam_re.astype(jnp.float32), -1e-4)
    li = lam_im.astype(jnp.float32)
    step = jnp.exp(log_step.astype(jnp.float32))[:, None]
    mag = jnp.exp(lr * step)
    ang = li * step
    abr = mag * jnp.cos(ang)
    abi = mag * jnp.sin(ang)
    nr = abr - 1.0
    ni = abi
    den = lr * lr + li * li
    cr = ((nr * lr + ni * li) / den)[..., None]
    ci = ((ni * lr - nr * li) / den)[..., None]
    br = b_re.astype(jnp.float32)
    bi = b_im.astype(jnp.float32)
    bbr = cr * br - ci * bi
    bbi = cr * bi + ci * br
    uf = u.astype(jnp.float32)
    bu_r = jnp.einsum('blgh,gph->blgp', uf, bbr)
    bu_i = jnp.einsum('blgh,gph->blgp', uf, bbi)
    a_r = jnp.broadcast_to(abr[None, None], (1, L) + abr.shape)
    a_i = jnp.broadcast_to(abi[None, None], (1, L) + abi.shape)
    _, _, sr, si = lax.associative_scan(_complex_affine_combine, (a_r, a_i, bu_r, bu_i), axis=1)
    y = (jnp.einsum('blgp,ghp->blgh', sr, c_re.astype(jnp.float32))
         - jnp.einsum('blgp,ghp->blgh', si, c_im.astype(jnp.float32))
         + d.astype(jnp.float32) * uf)
    g = jax.nn.gelu(y)
    gate = jnp.einsum('blgh,ghk->blgk', g, glu_w.astype(jnp.float32)) + glu_b.astype(jnp.float32)
    return g * jax.nn.sigmoid(gate)


def pool_mixer(v, pool_w, pool_scale):
    L = v.shape[1]
    vf = v.astype(jnp.float32)
    cs = jnp.cumsum(vf, axis=1)
    t = jnp.arange(1, L + 1, dtype=jnp.float32)
    outs = []
    for k, w in enumerate(POOL_WINDOWS):
        ck = cs[:, :, k]
        lower = jnp.pad(ck, ((0, 0), (w, 0), (0, 0)))[:, :L]
        cnt = jnp.minimum(t, float(w))[None, :, None]
        outs.append((ck - lower) / cnt - vf[:, :, k])
    p = jnp.stack(outs, axis=2)
    p = jnp.einsum('blkc,kcd->blkd', p, pool_w.astype(jnp.float32))
    return p * pool_scale.astype(jnp.float32)


def setup_inputs(seed: int = 0) -> dict:
    key = jax.random.key(seed)
    ks = jax.random.split(key, 24)
    f32 = jnp.float32
    G, H, P = SSM_GROUPS, SSM_GROUP, SSM_STATE
    n = jnp.arange(P, dtype=f32)
    x = jax.random.normal(ks[0], (BATCH, SEQ, D_MODEL), f32)
    meta_tokens = jax.random.normal(ks[1], (N_META, D_MODEL), f32)
    norm1_g = 1.0 + 0.02 * jax.random.normal(ks[2], (DEPTH, D_MODEL), f32)
    w_in = jax.random.normal(ks[3], (DEPTH, D_MODEL, D_MODEL), f32) * D_MODEL ** -0.5
    ssm_lambda_re = -0.5 + 0.01 * jax.random.normal(ks[4], (DEPTH, G, P), f32)
    ssm_lambda_im = math.pi * n + 0.01 * jax.random.normal(ks[5], (DEPTH, G, P), f32)
    ssm_log_step = jax.random.uniform(ks[6], (DEPTH, G), f32, math.log(STEP_MIN), math.log(STEP_MAX))
    ssm_b_re = jax.random.normal(ks[7], (DEPTH, G, P, H), f32) * (2.0 * H) ** -0.5
    ssm_b_im = jax.random.normal(ks[8], (DEPTH, G, P, H), f32) * (2.0 * H) ** -0.5
    ssm_c_re = jax.random.normal(ks[9], (DEPTH, G, H, P), f32) * (2.0 * P) ** -0.5 * 4.0
    ssm_c_im = jax.random.normal(ks[10], (DEPTH, G, H, P), f32) * (2.0 * P) ** -0.5 * 4.0
    ssm_d = jax.random.normal(ks[11], (DEPTH, G, H), f32)
    ssm_glu_w = jax.random.normal(ks[12], (DEPTH, G, H, H), f32) * H ** -0.5
    ssm_glu_b = 0.02 * jax.random.normal(ks[13], (DEPTH, G, H), f32)
    ssm_norm_g = 1.0 + 0.02 * jax.random.normal(ks[14], (DEPTH, D_SSM), f32)
    pool_w = jax.random.normal(ks[15], (DEPTH, POOL_GROUPS, POOL_GROUP_DIM, POOL_GROUP_DIM), f32) * POOL_GROUP_DIM ** -0.5
    pool_scale = 1.0 + 0.1 * jax.random.normal(ks[16], (DEPTH, POOL_GROUPS, POOL_GROUP_DIM), f32)
    pool_norm_g = 1.0 + 0.02 * jax.random.normal(ks[17], (DEPTH, D_POOL), f32)
    w_out = jax.random.normal(ks[18], (DEPTH, D_MODEL, D_MODEL), f32) * D_MODEL ** -0.5
    norm2_g = 1.0 + 0.02 * jax.random.normal(ks[19], (DEPTH, D_MODEL), f32)
    w_gate = jax.random.normal(ks[20], (DEPTH, D_MODEL, D_FF), f32) * D_MODEL ** -0.5
    w_up = jax.random.normal(ks[21], (DEPTH, D_MODEL, D_FF), f32) * D_MODEL ** -0.5
    w_down = jax.random.normal(ks[22], (DEPTH, D_FF, D_MODEL), f32) * D_FF ** -0.5
    final_norm_g = 1.0 + 0.02 * jax.random.normal(ks[23], (D_MODEL,), f32)
    return {"x": x, "meta_tokens": meta_tokens, "norm1_g": norm1_g, "w_in": w_in,
            "ssm_lambda_re": ssm_lambda_re, "ssm_lambda_im": ssm_lambda_im,
            "ssm_log_step": ssm_log_step, "ssm_b_re": ssm_b_re, "ssm_b_im": ssm_b_im,
            "ssm_c_re": ssm_c_re, "ssm_c_im": ssm_c_im, "ssm_d": ssm_d,
            "ssm_glu_w": ssm_glu_w, "ssm_glu_b": ssm_glu_b, "ssm_norm_g": ssm_norm_g,
            "pool_w": pool_w, "pool_scale": pool_scale, "pool_norm_g": pool_norm_g,
            "w_out": w_out, "norm2_g": norm2_g, "w_gate": w_gate, "w_up": w_up,
            "w_down": w_down, "final_norm_g": final_norm_g}


def reference(x, meta_tokens, norm1_g, w_in, ssm_lambda_re, ssm_lambda_im, ssm_log_step,
              ssm_b_re, ssm_b_im, ssm_c_re, ssm_c_im, ssm_d, ssm_glu_w, ssm_glu_b,
              ssm_norm_g, pool_w, pool_scale, pool_norm_g, w_out, norm2_g, w_gate, w_up,
              w_down, final_norm_g):
    B = x.shape[0]
    meta = jnp.broadcast_to(meta_tokens.astype(jnp.float32)[None], (B, N_META, D_MODEL))
    h = jnp.concatenate([meta, x.astype(jnp.float32)], axis=1)
    L = h.shape[1]
    for i in range(DEPTH):
        n1 = rmsnorm(h, norm1_g[i])
        proj = n1 @ w_in[i].astype(jnp.float32)
        u = proj[..., :D_SSM].reshape(B, L, SSM_GROUPS, SSM_GROUP)
        v = proj[..., D_SSM:].reshape(B, L, POOL_GROUPS, POOL_GROUP_DIM)
        ys = s5_mixer(u, ssm_lambda_re[i], ssm_lambda_im[i], ssm_log_step[i], ssm_b_re[i],
                      ssm_b_im[i], ssm_c_re[i], ssm_c_im[i], ssm_d[i], ssm_glu_w[i],
                      ssm_glu_b[i]).reshape(B, L, D_SSM)
        yp = pool_mixer(v, pool_w[i], pool_scale[i]).reshape(B, L, D_POOL)
        mixed = jnp.concatenate([rmsnorm(ys, ssm_norm_g[i]), rmsnorm(yp, pool_norm_g[i])], axis=-1)
        h = h + mixed @ w_out[i].astype(jnp.float32)
        n2 = rmsnorm(h, norm2_g[i])
        ff = jax.nn.silu(n2 @ w_gate[i].astype(jnp.float32)) * (n2 @ w_up[i].astype(jnp.float32))
        h = h + ff @ w_down[i].astype(jnp.float32)
    out = rmsnorm(h, final_norm_g)[:, N_META:]
    return out.astype(x.dtype)
```

```python
import math
from contextlib import ExitStack

import numpy as np
import concourse.bass as bass
import concourse.mybir as mybir
from concourse.bass_utils import run_bass_kernel_spmd

F32 = mybir.dt.float32
BF16 = mybir.dt.bfloat16
ALU = mybir.AluOpType
AF = mybir.ActivationFunctionType

D = 1024
SEQ = 4096
NMETA = 16
DFF = 2816
NJ = DFF // 128
NT = 512
NTILES = SEQ // NT
EPS = 1e-6
TWO_PI = 2.0 * math.pi
C1 = 6.28125
C2 = TWO_PI - C1


class Prog:
    COMPUTE = ("pe", "act", "dve", "pool")
    ALL = ("pe", "act", "dve", "pool", "sp")

    def __init__(self):
        self.ops = []
        self.last_w = {}
        self.readers = {}
        self.eng_pos = {e: 0 for e in self.ALL}
        self.chain = {}

    def add(self, eng, fn, reads=(), writes=(), dma_key=None, ndma=1):
        idx = len(self.ops)
        deps = set()
        for r in reads:
            if r in self.last_w:
                deps.add(self.last_w[r])
        for w in writes:
            if w in self.last_w:
                deps.add(self.last_w[w])
            for rd in self.readers.get(w, ()):
                deps.add(rd)
        pos = self.eng_pos[eng]
        self.eng_pos[eng] += 1
        keep = set()
        for d in deps:
            o = self.ops[d]
            if o["dma_key"] is None and o["eng"] == eng and dma_key is None:
                if eng == "pe":
                    continue
            keep.add(d)
        self.ops.append(dict(eng=eng, fn=fn, deps=keep, dma_key=dma_key, idx=idx, pos=pos,
                             signal=dma_key is not None, sigval=None, ndma=ndma))
        for r in reads:
            self.readers.setdefault(r, []).append(idx)
        for w in writes:
            self.last_w[w] = idx
            self.readers[w] = []
        return idx

    def barrier(self):
        keys = list(set(self.last_w) | set(self.readers))
        for e in self.ALL:
            self.add(e, lambda eng: eng.nop(), writes=keys)

    def emit(self, nc, final_wait_keys=()):
        ops = self.ops
        for o in ops:
            for d in o["deps"]:
                ops[d]["signal"] = True
        cnt = {e: 0 for e in self.ALL}
        dcnt = {}
        dma_keys = []
        for o in ops:
            if o["dma_key"] is not None:
                k = o["dma_key"]
                if k not in dcnt:
                    dcnt[k] = 0
                    dma_keys.append(k)
                dcnt[k] += 16 * o["ndma"]
                o["sigval"] = dcnt[k]
            elif o["signal"]:
                cnt[o["eng"]] += 1
                o["sigval"] = cnt[o["eng"]]
        with ExitStack() as st:
            sems = {}
            for e in self.ALL:
                sems[("eng", e)] = st.enter_context(nc.semaphore("s_" + e))
            for i, k in enumerate(dma_keys):
                sems[("dma", k)] = st.enter_context(nc.semaphore("d_%d" % i))
            for e in ("dve", "pool", "act"):
                self.chain[e] = [st.enter_context(nc.semaphore("c_" + e)), 0]
            block = st.enter_context(nc.Block())
            per_eng = {e: [o for o in ops if o["eng"] == e] for e in self.ALL}

            def semof(o):
                if o["dma_key"] is not None:
                    return sems[("dma", o["dma_key"])]
                return sems[("eng", o["eng"])]

            def body(engname):
                def f(eng):
                    known = {}
                    for o in per_eng[engname]:
                        need = {}
                        for d in o["deps"]:
                            od = ops[d]
                            s = semof(od)
                            v = od["sigval"]
                            if known.get(s.num, 0) >= v:
                                continue
                            if need.get(s.num, (None, 0))[1] < v:
                                need[s.num] = (s, v)
                        for s, v in need.values():
                            eng.wait_ge(s, v)
                            known[s.num] = v
                        ins = o["fn"](eng)
                        if o["signal"]:
                            if o["dma_key"] is not None:
                                il = ins if isinstance(ins, (list, tuple)) else [ins]
                                assert len(il) == o["ndma"]
                                for i_ in il:
                                    i_.then_inc(semof(o), 16)
                            else:
                                ins.then_inc(semof(o), 1)
                    if engname == "sp":
                        for k in final_wait_keys:
                            eng.wait_ge(sems[("dma", k)], dcnt[k])
                return f

            block.tensor(body("pe"))
            block.scalar(body("act"))
            block.vector(body("dve"))
            block.gpsimd(body("pool"))
            block.sync(body("sp"))


class Serial:
    def __init__(self, e, chain):
        self._e = e
        self._c = chain

    def __getattr__(self, name):
        fn = getattr(self._e, name)

        def w(*a, **k):
            ins = fn(*a, **k)
            self._c[1] += 1
            ins.then_inc(self._c[0], 1)
            self._e.wait_ge(self._c[0], self._c[1])
            return ins
        return w


class Spaced:
    def __init__(self, e, dummy):
        self._e = e
        self._d = dummy

    def __getattr__(self, name):
        fn = getattr(self._e, name)

        def w(*a, **k):
            ins = fn(*a, **k)
            self._e.memset(self._d, 0.0)
            return ins
        return w


CO_ID, CO_BD, CO_JM, CO_RME, CO_RMO, CO_PH, CO_SGN, CO_CME, CO_CMO = 0, 128, 256, 320, 321, 322, 323, 324, 836
CW = 1348


def make_consts():
    c = np.zeros((128, CW), np.float32)
    p = np.arange(128)
    c[:, CO_ID:CO_ID + 128] = np.eye(128)
    c[:, CO_BD:CO_BD + 128] = (p[:, None] // 16 == p[None, :] // 16)
    c[:, CO_JM:CO_JM + 64] = (p[:, None] % 64 == np.arange(64)[None, :])
    c[:, CO_RME] = ((p // 16) % 2 == 0)
    c[:, CO_RMO] = ((p // 16) % 2 == 1)
    c[:, CO_PH] = np.where(p < 64, math.pi / 2, 0.0)
    c[:, CO_SGN] = np.where(p < 64, 1.0, -1.0)
    col = np.arange(512)
    c[:, CO_CME:CO_CME + 512] = ((col // 16) % 2 == 0)[None, :]
    c[:, CO_CMO:CO_CMO + 512] = ((col // 16) % 2 == 1)[None, :]
    return c


DBG = {"level": 99, "ntiles": NTILES, "dumps": []}


def build_nc():
    LV = DBG["level"]
    nc = bass.Bass("TRN2", target_bir_lowering=False)

    def din(name, shape):
        return nc.dram_tensor(name, shape, F32, kind="ExternalInput").ap()

    x = din("x", [SEQ, D])
    meta = din("meta_tokens", [NMETA, D])
    cst_d = din("cst", [128, CW])
    norm1_g = din("norm1_g", [D])
    w_in = din("w_in", [D, D])
    lam_re = din("ssm_lambda_re", [32, 64])
    lam_im = din("ssm_lambda_im", [32, 64])
    log_step = din("ssm_log_step", [32])
    b_re = din("ssm_b_re", [32, 64, 16])
    b_im = din("ssm_b_im", [32, 64, 16])
    c_re = din("ssm_c_re", [32, 16, 64])
    c_im = din("ssm_c_im", [32, 16, 64])
    ssm_d = din("ssm_d", [512])
    glu_w = din("ssm_glu_w", [32, 16, 16])
    glu_b = din("ssm_glu_b", [512])
    ssm_ng = din("ssm_norm_g", [512])
    pool_w = din("pool_w", [4, 128, 128])
    pool_scale = din("pool_scale", [512])
    pool_ng = din("pool_norm_g", [512])
    w_out = din("w_out", [D, D])
    norm2_g = din("norm2_g", [D])
    w_gate = din("w_gate", [D, DFF])
    w_up = din("w_up", [D, DFF])
    w_down = din("w_down", [DFF, D])
    final_g = din("final_norm_g", [D])
    y = nc.dram_tensor("y", [SEQ, D], F32, kind="ExternalOutput").ap()
    win_t = nc.dram_tensor("win_t", [8, 128, 8, 128], BF16).ap()
    wout_t = nc.dram_tensor("wout_t", [16, 128, 512], BF16).ap()
    wg_t = nc.dram_tensor("wg_t", [NJ, 128, 8, 128], BF16).ap()
    wu_t = nc.dram_tensor("wu_t", [NJ, 128, 8, 128], BF16).ap()
    wd0_t = nc.dram_tensor("wd0_t", [NJ, 128, 512], BF16).ap()
    wd1_t = nc.dram_tensor("wd1_t", [NJ, 128, 512], BF16).ap()

    P = Prog()
    A = P.add

    with ExitStack() as st:
        def sb(name, shape, dt=F32):
            return st.enter_context(nc.sbuf_tensor(name, shape, dt))

        def psb(name):
            return st.enter_context(nc.psum_tensor(name, [128, 512], F32))

        B = [psb("B%d" % i) for i in range(8)]
        TP = B[7][:, 0:256].bitcast(BF16)
        TP2 = B[6][:, 0:256].bitcast(BF16)
        ST = B[7][:, 256:272]

        identb = sb("identb", [128, 128], BF16)
        WV = sb("WV", [128, 2, 4, 8, 128], BF16)
        WI = sb("WI", [128, 2, 8, 512], BF16)
        WL = sb("WL", [128, 4, 8, 128], BF16)
        AAm = sb("AAm", [128, 32, 6, 128], BF16)
        GW = sb("GW", [128, 4, 128], BF16)
        PW = sb("PW", [128, 4, 128], BF16)
        hb = sb("hb_sb", [128, 4])
        g1col = sb("g1col", [128, 8])
        g2col = sb("g2col", [128, 8])
        gocol = sb("gocol", [128, 8])
        pscale = sb("pscale", [128, 4])
        fgb = sb("fgb", [128, D])
        carry = [sb("carry0", [128, 32], BF16), sb("carry1", [128, 32], BF16)]
        vT = sb("vT", [128, 4, 16 + NT])

        with ExitStack() as s2:
            def tb(name, shape, dt=F32):
                return s2.enter_context(nc.sbuf_tensor(name, shape, dt))

            cst = tb("cst_sb", [128, CW])
            ident = cst[:, CO_ID:CO_ID + 128]
            bd = cst[:, CO_BD:CO_BD + 128]
            jm = cst[:, CO_JM:CO_JM + 64]
            A("sp", lambda e: e.dma_start(out=cst[:], in_=cst_d[:, :]), writes=["cst"], dma_key="cst")
            A("dve", lambda e: e.tensor_copy(identb[:], ident), reads=["cst"], writes=["identb"])

            def small_load(name, dst, src, n=1):
                def f(e):
                    with nc.allow_non_contiguous_dma(reason="tiny param"):
                        return e.dma_start(out=dst, in_=src)
                A("sp", f, writes=[name], dma_key=name)

            lre = tb("lre", [128, 32]); lim = tb("lim", [128, 32]); lst = tb("lst", [128, 32])
            small_load("lst", lst[:], log_step.partition_broadcast(128))
            bre = tb("bre", [128, 32, 16]); bim = tb("bim", [128, 32, 16])
            cre = tb("cre", [128, 32, 16]); cim = tb("cim", [128, 32, 16])
            BREK = ["bre%d_%d" % (h_, g_) for h_ in range(2) for g_ in range(4)]
            BIMK = ["bim%d_%d" % (h_, g_) for h_ in range(2) for g_ in range(4)]

            tl_n = [0]

            def tload(src2d, R, dst, key):
                i = tl_n[0]
                tl_n[0] += 1
                nat = tb("nat%d" % i, [128, 128])
                A("sp", lambda e: [e.dma_start(out=nat[0:R, 0:64], in_=src2d), e.dma_start(out=nat[0:R, 64:128], in_=src2d)], writes=["nat%d" % i], dma_key="nat%d" % i, ndma=2)
                bank = i % 4
                A("pe", lambda e: e.transpose(B[bank][:, 0:R], nat[0:R, :], ident[0:R, 0:R]), reads=["nat%d" % i, "cst"], writes=["B%d" % bank])
                A("act", lambda e: e.activation(out=dst, in_=B[bank][:, 0:R], func=AF.Copy), reads=["B%d" % bank], writes=[key])
            tload(lam_re[:, :], 32, lre[:], "lre")
            tload(lam_im[:, :], 32, lim[:], "lim")
            c2r = c_re.rearrange("g h p -> (g h) p")
            c2i = c_im.rearrange("g h p -> (g h) p")
            for q in range(4):
                tload(c2r[q * 128:(q + 1) * 128, :], 128, cre[:, q * 8:(q + 1) * 8, :].rearrange("p g h -> p (g h)"), "cre%d" % q)
                tload(c2i[q * 128:(q + 1) * 128, :], 128, cim[:, q * 8:(q + 1) * 8, :].rearrange("p g h -> p (g h)"), "cim%d" % q)
            CREK = ["cre%d" % q for q in range(4)]
            CIMK = ["cim%d" % q for q in range(4)]
            for hlf in range(2):
                for gq in range(4):
                    sl = slice(64 * hlf, 64 * hlf + 64)
                    gs = slice(8 * gq, 8 * gq + 8)
                    small_load("bre%d" % hlf + "_%d" % gq, bre[sl, gs, :], b_re[gs].rearrange("g p h -> p g h"))
                    small_load("bim%d" % hlf + "_%d" % gq, bim[sl, gs, :], b_im[gs].rearrange("g p h -> p g h"))
            def conv(key, pairs):
                def f(e):
                    return [e.dma_start(out=d_, in_=s_) for d_, s_ in pairs]
                A("pool", f, reads=["lre", "lim", "lst"] + BREK + BIMK + CREK + CIMK, writes=[key], dma_key=key, ndma=len(pairs))
            conv("cv_win", [(win_t[m], w_in[:, m * 128:(m + 1) * 128].rearrange("(k p) c -> p k c", p=128)) for m in range(8)])
            wo_order0 = [(br, h, c4) for br in range(2) for h in range(2) for c4 in range(4)]
            conv("cv_wout", [(wout_t[i_], w_out[(br * 4 + c4) * 128:(br * 4 + c4 + 1) * 128, h * 512:(h + 1) * 512]) for i_, (br, h, c4) in enumerate(wo_order0)])
            conv("cv_wg", [(wg_t[j], w_gate[:, j * 128:(j + 1) * 128].rearrange("(k p) c -> p k c", p=128)) for j in range(NJ)])
            conv("cv_wu", [(wu_t[j], w_up[:, j * 128:(j + 1) * 128].rearrange("(k p) c -> p k c", p=128)) for j in range(NJ)])
            conv("cv_wd0", [(wd0_t[j], w_down[j * 128:(j + 1) * 128, 0:512]) for j in range(NJ)])
            conv("cv_wd1", [(wd1_t[j], w_down[j * 128:(j + 1) * 128, 512:1024]) for j in range(NJ)])

            LK = ["lre", "lim", "lst"]
            dcol = tb("dcol", [128, 4])
            small_load("dcol", dcol[:], ssm_d.rearrange("(q p) -> p q", p=128))
            small_load("hb", hb[:], glu_b.rearrange("(q p) -> p q", p=128))
            small_load("g1col", g1col[:], norm1_g.rearrange("(k p) -> p k", p=128))
            small_load("g2col", g2col[:], norm2_g.rearrange("(k p) -> p k", p=128))
            small_load("gocolA", gocol[:, 0:4], ssm_ng.rearrange("(k p) -> p k", p=128))
            small_load("gocolB", gocol[:, 4:8], pool_ng.rearrange("(k p) -> p k", p=128))
            small_load("pscale", pscale[:], pool_scale.rearrange("(k p) -> p k", p=128))
            small_load("fgb", fgb[:], final_g.partition_broadcast(128))
            gwr = tb("gwr", [128, 4, 8, 16])
            for q in range(4):
                src = bass.AP(glu_w.tensor, q * 2048, [[16, 128], [0, 8], [1, 16]])
                small_load("gwr%d" % q, gwr[:, q], src)
            A("pool", lambda e: e.dma_start(out=PW[:], in_=pool_w.rearrange("k c d -> c k d")), writes=["PW"], dma_key="PW")
            T = {n: tb("t_" + n, [128, 32]) for n in ("step", "lr", "z", "mag", "ang", "r", "m", "r2", "sn", "cs", "Ar", "Ai", "den", "nr", "cr", "ci", "u1", "u2", "u3")}

            dum = tb("dum", [128, 64])

            def dv(fn, reads, writes):
                def g(e):
                    fn(Serial(e, P.chain["dve"]))
                    return e.nop()
                A("dve", g, reads=reads, writes=writes)

            def exp_taylor(dst, src, nsq, deg, key):
                u = T["u1"]
                fact = [1.0]
                for k in range(1, deg + 1):
                    fact.append(fact[-1] * k)

                def f(e):
                    e.tensor_scalar(out=u[:], in0=src[:], scalar1=1.0 / (2 ** nsq), scalar2=None, op0=ALU.mult)
                    cur_, oth_ = T["u2"], dst
                    e.tensor_scalar(out=cur_[:], in0=u[:], scalar1=1.0 / fact[deg], scalar2=None, op0=ALU.mult)
                    for k in range(deg - 1, 0, -1):
                        e.scalar_tensor_tensor(out=oth_[:], in0=cur_[:], scalar=1.0 / fact[k], in1=u[:], op0=ALU.add, op1=ALU.mult)
                        cur_, oth_ = oth_, cur_
                    e.tensor_scalar(out=oth_[:], in0=cur_[:], scalar1=1.0, scalar2=None, op0=ALU.add)
                    cur_, oth_ = oth_, cur_
                    for _ in range(nsq):
                        e.tensor_tensor(out=oth_[:], in0=cur_[:], in1=cur_[:], op=ALU.mult)
                        cur_, oth_ = oth_, cur_
                    if cur_ is not dst:
                        e.tensor_copy(dst[:], cur_[:])
                    return None
                return f
            dv(exp_taylor(T["step"], lst, 3, 12, "step"), LK, ["step"])

            def f_lr(e):
                e.tensor_scalar(out=T["lr"][:], in0=lre[:], scalar1=-1e-4, scalar2=None, op0=ALU.min)
                e.tensor_tensor(out=T["ang"][:], in0=lim[:], in1=T["step"][:], op=ALU.mult)
                return e.tensor_tensor(out=T["z"][:], in0=T["lr"][:], in1=T["step"][:], op=ALU.mult)
            dv(f_lr, LK + ["step"], ["lr", "z", "ang"])
            dv(exp_taylor(T["mag"], T["z"], 0, 10, "mag"), ["z"], ["mag"])

            def f_rr(e):
                e.tensor_scalar(out=T["r"][:], in0=T["ang"][:], scalar1=cst[:, CO_PH:CO_PH + 1], scalar2=None, op0=ALU.add)
                for it in range(6):
                    e.tensor_scalar(out=T["m"][:], in0=T["r"][:], scalar1=math.pi, scalar2=None, op0=ALU.is_gt)
                    e.scalar_tensor_tensor(out=T["r2"][:], in0=T["m"][:], scalar=-C1, in1=T["r"][:], op0=ALU.mult, op1=ALU.add)
                    e.scalar_tensor_tensor(out=T["r"][:], in0=T["m"][:], scalar=-C2, in1=T["r2"][:], op0=ALU.mult, op1=ALU.add)
                for it in range(1):
                    e.tensor_scalar(out=T["m"][:], in0=T["r"][:], scalar1=-math.pi, scalar2=None, op0=ALU.is_lt)
                    e.scalar_tensor_tensor(out=T["r2"][:], in0=T["m"][:], scalar=C1, in1=T["r"][:], op0=ALU.mult, op1=ALU.add)
                    e.scalar_tensor_tensor(out=T["r"][:], in0=T["m"][:], scalar=C2, in1=T["r2"][:], op0=ALU.mult, op1=ALU.add)
                return None
            dv(f_rr, ["ang", "cst"], ["r"])
            A("act", lambda e: e.activation(out=T["sn"][:], in_=T["r"][:], func=AF.Sin), reads=["r"], writes=["sn"])
            def f_rr2(e):
                e.tensor_scalar(out=T["r2"][:], in0=cst[:, CO_PH:CO_PH + 1].to_broadcast([128, 32]), scalar1=-2.0, scalar2=math.pi / 2, op0=ALU.mult, op1=ALU.add)
                e.tensor_tensor(out=T["u1"][:], in0=T["r"][:], in1=T["r2"][:], op=ALU.add)
                e.tensor_scalar(out=T["m"][:], in0=T["u1"][:], scalar1=math.pi, scalar2=None, op0=ALU.is_gt)
                e.scalar_tensor_tensor(out=T["u2"][:], in0=T["m"][:], scalar=-C1, in1=T["u1"][:], op0=ALU.mult, op1=ALU.add)
                e.tensor_copy(T["u3"][:], T["m"][:])
                e.scalar_tensor_tensor(out=T["u1"][:], in0=T["u3"][:], scalar=-C2, in1=T["u2"][:], op0=ALU.mult, op1=ALU.add)
                e.tensor_scalar(out=T["m"][:], in0=T["u1"][:], scalar1=-math.pi, scalar2=None, op0=ALU.is_lt)
                e.scalar_tensor_tensor(out=T["u2"][:], in0=T["m"][:], scalar=C1, in1=T["u1"][:], op0=ALU.mult, op1=ALU.add)
                e.tensor_copy(T["u3"][:], T["m"][:])
                return e.scalar_tensor_tensor(out=T["r2"][:], in0=T["u3"][:], scalar=C2, in1=T["u2"][:], op0=ALU.mult, op1=ALU.add)
            dv(f_rr2, ["r", "cst"], ["r2"])
            A("act", lambda e: e.activation(out=T["cs"][:], in_=T["r2"][:], func=AF.Sin), reads=["r2"], writes=["cs"])
            def f_a(e):
                e.tensor_tensor(out=T["Ar"][0:64, :], in0=T["mag"][0:64, :], in1=T["sn"][0:64, :], op=ALU.mult)
                e.tensor_tensor(out=T["Ai"][0:64, :], in0=T["mag"][0:64, :], in1=T["cs"][0:64, :], op=ALU.mult)
                e.tensor_tensor(out=T["Ar"][64:128, :], in0=T["mag"][64:128, :], in1=T["cs"][64:128, :], op=ALU.mult)
                e.tensor_tensor(out=T["Ai"][64:128, :], in0=T["mag"][64:128, :], in1=T["sn"][64:128, :], op=ALU.mult)
                e.tensor_tensor(out=T["u1"][:], in0=T["lr"][:], in1=T["lr"][:], op=ALU.mult)
                e.tensor_tensor(out=T["u2"][:], in0=lim[:], in1=lim[:], op=ALU.mult)
                e.tensor_scalar(out=T["nr"][:], in0=T["Ar"][:], scalar1=-1.0, scalar2=None, op0=ALU.add)
                e.tensor_tensor(out=T["den"][:], in0=T["u1"][:], in1=T["u2"][:], op=ALU.add)
                e.tensor_tensor(out=T["u1"][:], in0=T["nr"][:], in1=T["lr"][:], op=ALU.mult)
                e.reciprocal(out=T["u3"][:], in_=T["den"][:])
                e.tensor_tensor(out=T["u2"][:], in0=T["Ai"][:], in1=lim[:], op=ALU.mult)
                e.tensor_tensor(out=T["cr"][:], in0=T["u1"][:], in1=T["u2"][:], op=ALU.add)
                e.tensor_tensor(out=T["u1"][:], in0=T["Ai"][:], in1=T["lr"][:], op=ALU.mult)
                e.tensor_tensor(out=T["u2"][:], in0=T["nr"][:], in1=lim[:], op=ALU.mult)
                e.tensor_tensor(out=T["ci"][:], in0=T["u1"][:], in1=T["u2"][:], op=ALU.subtract)
                e.tensor_tensor(out=T["den"][:], in0=T["cr"][:], in1=T["u3"][:], op=ALU.mult)
                e.tensor_tensor(out=T["nr"][:], in0=T["ci"][:], in1=T["u3"][:], op=ALU.mult)
                e.tensor_copy(T["cr"][:], T["den"][:])
                return e.tensor_copy(T["ci"][:], T["nr"][:])
            dv(f_a, ["mag", "sn", "cs", "lr"] + LK, ["a", "zoh"])

            Pr = tb("Pr", [128, 9, 32]); Pi = tb("Pi", [128, 9, 32])
            Qr = tb("Qr", [128, 6, 32]); Qi = tb("Qi", [128, 6, 32])
            w1 = tb("w1", [128, 4, 32]); w2 = tb("w2", [128, 4, 32])

            def cmul(e, outr, outi, ar, ai, br, bi, shape):
                s1 = w1[:, 0:shape] if shape else w1[:, 0]
                s2_ = w2[:, 0:shape] if shape else w2[:, 0]
                e.tensor_tensor(out=s1, in0=ar, in1=br, op=ALU.mult)
                e.tensor_tensor(out=s2_, in0=ai, in1=bi, op=ALU.mult)
                e.tensor_tensor(out=outr, in0=s1, in1=s2_, op=ALU.subtract)
                e.tensor_tensor(out=s1, in0=ar, in1=bi, op=ALU.mult)
                e.tensor_tensor(out=s2_, in0=ai, in1=br, op=ALU.mult)
                return e.tensor_tensor(out=outi, in0=s1, in1=s2_, op=ALU.add)

            def f_pow(e):
                e.memset(Pr[:, 0, :], 1.0)
                e.memset(Pi[:, 0, :], 0.0)
                e.tensor_copy(Pr[:, 1, :], T["Ar"][:])
                e.tensor_copy(Pi[:, 1, :], T["Ai"][:])
                cmul(e, Pr[:, 2, :], Pi[:, 2, :], Pr[:, 1, :], Pi[:, 1, :], Pr[:, 1, :], Pi[:, 1, :], 0)
                bc = lambda t, k, n: t[:, k:k + 1, :].to_broadcast([128, n, 32])
                cmul(e, Pr[:, 3:5, :], Pi[:, 3:5, :], Pr[:, 1:3, :], Pi[:, 1:3, :], bc(Pr, 2, 2), bc(Pi, 2, 2), 2)
                cmul(e, Pr[:, 5:9, :], Pi[:, 5:9, :], Pr[:, 1:5, :], Pi[:, 1:5, :], bc(Pr, 4, 4), bc(Pi, 4, 4), 4)
                e.tensor_copy(Qr[:, 0, :], Pr[:, 8, :])
                ins = e.tensor_copy(Qi[:, 0, :], Pi[:, 8, :])
                for k in range(5):
                    ins = cmul(e, Qr[:, k + 1, :], Qi[:, k + 1, :], Qr[:, k, :], Qi[:, k, :], Qr[:, k, :], Qi[:, k, :], 0)
                return ins
            dv(f_pow, ["a"], ["P", "Q"])

            Bbr = tb("Bbr", [128, 32, 16]); Bbi = tb("Bbi", [128, 32, 16])
            x1 = tb("x1", [128, 8, 32, 16]); x2 = tb("x2", [128, 8, 32, 16])
            bcg = lambda t: t[:].unsqueeze(2).to_broadcast([128, 32, 16])

            def f_bbar(e):
                e.tensor_tensor(out=x1[:, 0], in0=bre[:], in1=bcg(T["cr"]), op=ALU.mult)
                e.tensor_tensor(out=x2[:, 0], in0=bim[:], in1=bcg(T["ci"]), op=ALU.mult)
                e.tensor_tensor(out=Bbr[:], in0=x1[:, 0], in1=x2[:, 0], op=ALU.subtract)
                e.tensor_tensor(out=x1[:, 1], in0=bim[:], in1=bcg(T["cr"]), op=ALU.mult)
                e.tensor_tensor(out=x2[:, 1], in0=bre[:], in1=bcg(T["ci"]), op=ALU.mult)
                return e.tensor_tensor(out=Bbi[:], in0=x1[:, 1], in1=x2[:, 1], op=ALU.add)
            dv(f_bbar, ["zoh"] + BREK + BIMK, ["Bb"])

            BKs = tb("BKs", [128, 8, 32, 16])
            CIs = tb("CIs", [128, 8, 32, 16])
            CI0 = tb("CI0", [128, 32, 16])
            pk = lambda t, k0: t[:, k0:k0 + 8, :].unsqueeze(3).to_broadcast([128, 8, 32, 16])
            bk = lambda t: t[:].unsqueeze(1).to_broadcast([128, 8, 32, 16])

            def f_stack_c(e):
                lo, hi = slice(0, 64), slice(64, 128)
                e.tensor_copy(CI0[lo], cre[lo])
                e.tensor_scalar(out=CI0[hi], in0=cim[hi], scalar1=-1.0, scalar2=None, op0=ALU.mult)
                e.tensor_scalar(out=x1[lo, 0], in0=cim[lo], scalar1=-1.0, scalar2=None, op0=ALU.mult)
                e.tensor_scalar(out=cim[hi], in0=cre[hi], scalar1=-1.0, scalar2=None, op0=ALU.mult)
                return e.tensor_copy(cim[lo], x1[lo, 0])
            dv(f_stack_c, CREK + CIMK, ["Cs2", "CI0", "x"])

            def f_ci(e):
                e.tensor_tensor(out=x1[:], in0=bk(CI0), in1=pk(Pr, 1), op=ALU.mult)
                e.tensor_tensor(out=x2[:], in0=bk(cim), in1=pk(Pi, 1), op=ALU.mult)
                return e.tensor_tensor(out=CIs[:], in0=x1[:], in1=x2[:], op=ALU.add)
            dv(f_ci, ["x", "P", "Cs2", "CI0"], ["CIs", "x"])

            def f_stack_b(e):
                lo, hi = slice(0, 64), slice(64, 128)
                e.tensor_copy(x1[hi, 0], Bbr[hi])
                e.tensor_copy(Bbr[hi], Bbi[hi])
                e.tensor_copy(Bbi[hi], x1[hi, 0])
                pscr = x2[lo, 0].rearrange("p g h -> p (g h)")[:, 0:288].rearrange("p (k g) -> p k g", g=32)
                e.tensor_scalar(out=pscr, in0=Pi[lo], scalar1=-1.0, scalar2=None, op0=ALU.mult)
                return e.tensor_copy(Pi[lo], pscr)
            dv(f_stack_b, ["Bb", "P", "CIs", "x"], ["stk", "x", "Pneg"])

            def f_bk(e):
                e.tensor_tensor(out=x1[:], in0=bk(Bbr), in1=pk(Pr, 0), op=ALU.mult)
                e.tensor_tensor(out=x2[:], in0=bk(Bbi), in1=pk(Pi, 0), op=ALU.mult)
                return e.tensor_tensor(out=BKs[:], in0=x1[:], in1=x2[:], op=ALU.add)
            dv(f_bk, ["stk", "P", "Pneg"], ["BKs", "x"])

            def f_wi(e):
                for par in range(2):
                    cm = cst[:, (CO_CME, CO_CMO)[par]:(CO_CME, CO_CMO)[par] + 512].unsqueeze(1).to_broadcast([128, 8, 512])
                    ins = e.tensor_tensor(out=WI[:, par], in0=CIs[:].rearrange("p j g h -> p j (g h)"), in1=cm, op=ALU.mult)
                return ins
            dv(f_wi, ["CIs", "cst"], ["WI"])

            col1 = tb("col1", [128, 6, 32], BF16); col2 = tb("col2", [128, 6, 32], BF16)
            jmb = tb("jmb", [128, 64], BF16)

            def f_aa(e):
                e.tensor_copy(col1[0:64], Qr[0:64])
                e.tensor_scalar(out=col1[64:128], in0=Qi[64:128], scalar1=-1.0, scalar2=None, op0=ALU.mult)
                e.tensor_copy(col2[0:64], Qi[0:64])
                e.tensor_copy(col2[64:128], Qr[64:128])
                e.tensor_copy(jmb[:], jm)
                ins = None
                for k in range(6):
                    jb = jmb[:].unsqueeze(1).to_broadcast([128, 32, 64])
                    e.tensor_tensor(out=AAm[:, :, k, 0:64], in0=jb, in1=col1[:, k, :].unsqueeze(2).to_broadcast([128, 32, 64]), op=ALU.mult)
                    ins = e.tensor_tensor(out=AAm[:, :, k, 64:128], in0=jb, in1=col2[:, k, :].unsqueeze(2).to_broadcast([128, 32, 64]), op=ALU.mult)
                return ins
            dv(f_aa, ["Q", "cst"], ["AA"])

            A("dve", lambda e: e.tensor_scalar(out=hb[:], in0=hb[:], scalar1=0.5, scalar2=None, op0=ALU.mult), reads=["hb"], writes=["hb"])
            def gwf(e):
                for q in range(4):
                    ins = e.scalar_tensor_tensor(out=GW[:, q, :], in0=gwr[:, q].rearrange("p a b -> p (a b)"), scalar=1.0, in1=bd, op0=ALU.mult, op1=ALU.mult)
                return ins
            A("dve", gwf, reads=["gwr0", "gwr1", "gwr2", "gwr3", "cst"], writes=["GW"])

            tmpk = tb("tmpk", [128, 128])
            for q in range(4):
                for kh in range(2):
                    idx = q * 2 + kh
                    bm = B[(idx % 2) * 2]
                    bt = B[(idx % 2) * 2 + 1]
                    km, kt = "B%d" % ((idx % 2) * 2), "B%d" % ((idx % 2) * 2 + 1)
                    ci0 = CI0[:, q * 8:(q + 1) * 8, :].rearrange("p g h -> p (g h)")

                    def f_pe(e, bm=bm, bt=bt, q=q, kh=kh, ci0=ci0):
                        ins = None
                        for i in range(4):
                            k = kh * 4 + i
                            blk = BKs[:, k, q * 8:(q + 1) * 8, :].rearrange("p g h -> p (g h)")
                            e.matmul(bm[:, i * 128:(i + 1) * 128], lhsT=blk, rhs=ci0, start=True, stop=True)
                            ins = e.transpose(bt[:, i * 128:(i + 1) * 128], blk, ident)
                        return ins
                    A("pe", f_pe, reads=["BKs", "CI0", "cst"], writes=[km, kt])
                    bd4 = bd.unsqueeze(1).to_broadcast([128, 4, 128])
                    if kh == 0:
                        A("dve", lambda e, bm=bm: e.tensor_tensor(out=tmpk[:], in0=bm[:, 0:128], in1=bd, op=ALU.mult), reads=[km, "cst"], writes=["tmpk"])
                        A("dve", lambda e, bm=bm, q=q: e.tensor_tensor(out=WL[:, q, 1:4, :], in0=bm[:, 128:512].rearrange("p (i c) -> p i c", i=3), in1=bd.unsqueeze(1).to_broadcast([128, 3, 128]), op=ALU.mult),
                          reads=[km, "cst"], writes=[("WL", q, kh)])
                        A("dve", lambda e, q=q: e.scalar_tensor_tensor(out=WL[:, q, 0, :], in0=ident, scalar=dcol[:, q:q + 1], in1=tmpk[:], op0=ALU.mult, op1=ALU.add),
                          reads=["tmpk", "cst", "dcol"], writes=[("WL0", q)])
                    else:
                        A("dve", lambda e, bm=bm, q=q, kh=kh, bd4=bd4: e.tensor_tensor(out=WL[:, q, kh * 4:kh * 4 + 4, :], in0=bm[:].rearrange("p (i c) -> p i c", i=4), in1=bd4, op=ALU.mult),
                          reads=[km, "cst"], writes=[("WL", q, kh)])
                    for par in range(2):
                        A("act", lambda e, bt=bt, q=q, kh=kh, par=par: e.mul(out=WV[:, par, q, kh * 4:kh * 4 + 4, :], in_=bt[:].rearrange("p (i c) -> p i c", i=4),
                                                                           mul=cst[:, (CO_RME, CO_RMO)[par]:(CO_RME, CO_RMO)[par] + 1]),
                          reads=[kt, "cst"], writes=[("WV", q, kh, par)])

            if DBG.get("setup_dumps"):
                for nm, ap in [("step", T["step"][:]), ("lre", lre[:]), ("lim", lim[:]), ("lst", lst[:]), ("Ar", T["Ar"][:]), ("Ai", T["Ai"][:]), ("cr", T["cr"][:]), ("mag", T["mag"][:]), ("sn", T["sn"][:]), ("r", T["r"][:])]:
                    dd = nc.dram_tensor("sd_" + nm, [128, 32], F32, kind="ExternalOutput").ap()
                    A("sp", lambda e, dd=dd, ap=ap: e.dma_start(out=dd[:, :], in_=ap), reads=list(set(P.last_w)), dma_key="sd_" + nm)
                    DBG.setdefault("sd_keys", []).append("sd_" + nm)
                for nm, ap in [("cre", cre), ("cim", cim), ("bre", bre), ("bim", bim), ("Bbr", Bbr), ("CI0", CI0)]:
                    dd = nc.dram_tensor("sd_" + nm, [128, 512], F32, kind="ExternalOutput").ap()
                    A("sp", lambda e, dd=dd, ap=ap: e.dma_start(out=dd[:, :], in_=ap[:].rearrange("p g h -> p (g h)")), reads=list(set(P.last_w)), dma_key="sd_" + nm)
                    DBG.setdefault("sd_keys", []).append("sd_" + nm)
            A("pool", lambda e: e.memset(carry[0][:], 0.0), writes=["carry0"])
            A("pool", lambda e: e.memset(vT[:, :, 0:16], 0.0), writes=["halo"])

        P.barrier()

        xt = sb("xt", [128, 4, D])
        nbf = [sb("nbf0", [128, D], BF16)] * 2
        junk = nbf[0]
        nT = sb("nT", [128, 8, NT], BF16)
        pdum = sb("pdum", [128, 64])
        uT = sb("uT", [128, 4, NT], BF16)
        ss = sb("ss", [128, 16])
        rs = sb("rs", [128, 16])
        uni = sb("uni", [128, 5632])
        E32 = uni[:, 0:2048].rearrange("p (g c) -> p g c", c=64)
        Ebf = uni[:, 2048:3072].bitcast(BF16).rearrange("p (g c) -> p g c", c=64)
        sq = [uni[:, 3072:3584], uni[:, 3584:4096]]
        inn = [uni[:, 4096:4608], uni[:, 4608:5120]]
        th = [uni[:, 5120:5632], sb("th1", [128, NT])[:]]
        ffT = uni[:, :].bitcast(BF16).rearrange("p (j n) -> p j n", n=NT)
        ost = uni[:, 0:4096].rearrange("p (s d) -> p s d", d=D)
        UNIK = [("E32", b_) for b_ in range(4)] + [("Ebf", b_) for b_ in range(4)] + ["sq0", "sq1", "inn0", "inn1", "th0"] + [("ffT", j_) for j_ in range(NJ)]
        Sprev = sb("Sprev", [128, 32, 64], BF16)
        pa = sb("pa", [128, 16 + NT]); pb_ = sb("pb", [128, 16 + NT]); pc = sb("pc", [128, 16 + NT])
        pbf = [sb("pbf0", [128, NT], BF16)] * 2
        g2 = [sb("g2%d" % i, [128, NT]) for i in range(2)]
        g2b = [sb("g2b0", [128, NT], BF16)] * 2
        gbf = [g2b[0], sb("gbf1", [128, NT], BF16)]
        mixT = nT
        sqb = [sb("sqb0", [128, NT], BF16)] * 2
        sil = g2
        NR = 3
        NRD1 = 4
        NWO = 5
        NWI = 4
        wring = [sb("wr%d" % i, [128, 8, 128], BF16) for i in range(NWI)]
        woring = [sb("wo%d" % i, [128, 512], BF16) for i in range(NWO)]
        wgr = [sb("wg%d" % i, [128, 8, 128], BF16) for i in range(NR)]
        wur = [sb("wu%d" % i, [128, 8, 128], BF16) for i in range(NR)]
        wdr = [sb("wd%d" % i, [128, 512], BF16) for i in range(NR)]
        wd1r = [sb("wdd%d" % i, [128, 512], BF16) for i in range(NRD1)]
        ones = sb("ones", [128, 1], BF16)
        A("pool", lambda e: e.memset(ones[:], 1.0), writes=["ones"])
        A("pool", lambda e: e.memset(ss[:, 12:14], 1.0), writes=["warm"])

        cnt = {"win": 0, "wout": 0, "ffn": 0, "wd1": 0}
        win_issued = set()

        def emit_win(gi):
            win_issued.add(gi)
            m = gi % 8
            slot = gi % NWI
            A("sp", lambda e, slot=slot, m=m: e.dma_start(out=wring[slot][:], in_=win_t[m]), reads=["cv_win"], writes=["wr%d" % slot], dma_key="wr%d" % slot)


        def rmsnorm_to_T(npart, src_of, gcol, nsub, tag, eps):
            for s in range(nsub):
                A("act", lambda e, s=s: e.activation(out=junk[0:npart, :], in_=src_of(s), func=AF.Square, accum_out=ss[0:npart, s:s + 1]),
                  reads=[("xt", s)], writes=[("ss", s), "nbf0"])
            A("dve", lambda e: e.tensor_scalar(out=rs[0:npart, 0:nsub], in0=ss[0:npart, 0:nsub], scalar1=1.0 / D, scalar2=eps, op0=ALU.mult, op1=ALU.add), reads=[("ss", s_) for s_ in range(nsub)], writes=["rs"])
            A("act", lambda e: e.activation(out=ss[0:npart, 0:nsub], in_=rs[0:npart, 0:nsub], func=AF.Sqrt), reads=["rs"], writes=[("ss", s_) for s_ in range(nsub)])
            A("dve", lambda e: e.reciprocal(out=rs[0:npart, 0:nsub], in_=ss[0:npart, 0:nsub]), reads=[("ss", s_) for s_ in range(nsub)], writes=["rs"])
            nb2 = g2[0][:].bitcast(BF16)
            for s in range(nsub):
                nb, nbk = (nbf[0][:], "nbf0") if s % 2 == 0 else (nb2, "g20")
                A("act", lambda e, s=s, nb=nb: e.mul(out=nb[0:npart, :], in_=src_of(s), mul=rs[0:npart, s:s + 1]),
                  reads=["rs", ("xt", s)], writes=[nbk])
                for kb in range(2):
                    tp, tpk = (TP, "TP") if kb == 0 else (TP2, "B6")

                    def f_tr(e, s=s, kb=kb, nb=nb, tp=tp):
                        ins = None
                        for i in range(4):
                            k = kb * 4 + i
                            ins = e.transpose(tp[:, i * npart:(i + 1) * npart], nb[0:npart, k * 128:(k + 1) * 128], identb[0:npart, 0:npart])
                        return ins
                    A("pe", f_tr, reads=[nbk, "identb"], writes=[tpk])
                    A("dve", lambda e, s=s, kb=kb, tp=tp: e.tensor_tensor(out=nT[:, kb * 4:kb * 4 + 4, s * 128:s * 128 + npart], in0=tp[:, 0:4 * npart].rearrange("p (i n) -> p i n", i=4),
                                                                 in1=gcol[:, kb * 4:kb * 4 + 4].unsqueeze(2).to_broadcast([128, 4, npart]), op=ALU.mult),
                      reads=[tpk, tag], writes=[("nT", kb * 4 + i) for i in range(4)])

        def s5_scan(ntok, cur, nxt, has_carry, hooks=None):
            nch = ntok // 8
            steps = [s_ for s_ in (1, 2, 4, 8, 16, 32) if s_ < nch]
            def f_v(e):
                ins = None
                for q in range(4):
                    for par in range(2):
                        sl = q * 2 + par
                        for i in range(8):
                            for b in range(4):
                                u3 = uT[32 * b:32 * b + 32, q, 0:ntok].rearrange("p (c i) -> p c i", i=8)
                                ins = e.matmul(B[b][:, sl * 64:sl * 64 + nch], lhsT=WV[32 * b:32 * b + 32, par, q, 7 - i, :], rhs=u3[:, :, i],
                                               start=(i == 0), stop=(i == 7 and not has_carry), tile_position=(32 * b, 0))
                        if has_carry:
                            for b in range(4):
                                g = q * 8 + 2 * b + par
                                ins = e.matmul(B[b][:, sl * 64:sl * 64 + 1], lhsT=AAm[:, g, 0, :], rhs=cur[:, g:g + 1], start=False, stop=True)
                return ins
            A("pe", f_v, reads=[("uT", q_) for q_ in range(4)] + ["carry_cur"], writes=["B0", "B1", "B2", "B3"])
            if hooks and -1 in hooks:
                hooks[-1]()

            if DBG.get("scan_parts", 9) < 2 and ntok == NT:
                return

            def gview(t, b, lo, hi):
                return t[:, :, lo:hi].rearrange("p (q r) c -> p q r c", r=8)[:, :, 2 * b:2 * b + 2, :]

            def bview(b, lo, hi):
                return B[b][:].rearrange("p (q r c) -> p q r c", q=4, r=2)[:, :, :, lo:hi]
            for b in range(4):
                A("dve", lambda e, b=b: e.tensor_copy(gview(E32, b, 0, nch), bview(b, 0, nch)), reads=["B%d" % b], writes=[("E32", b)])
                A("act", lambda e, b=b: e.activation(out=gview(Ebf, b, 0, nch), in_=gview(E32, b, 0, nch), func=AF.Copy), reads=[("E32", b)], writes=[("Ebf", b)])
            if DBG.get("scan_parts", 9) < 3 and ntok == NT:
                return
            for si, sh in enumerate(steps):
                for b in range(4):
                    def f_l(e, b=b, si=si, sh=sh):
                        ins = None
                        for q in range(4):
                            for par in range(2):
                                g = q * 8 + 2 * b + par
                                sl = q * 2 + par
                                ins = e.matmul(B[b][:, sl * 64 + sh:sl * 64 + nch], lhsT=AAm[:, g, si, :], rhs=Ebf[:, g, 0:nch - sh], start=True, stop=True)
                        return ins
                    A("pe", f_l, reads=[("Ebf", b)], writes=["B%d" % b])
                    A("dve", lambda e, b=b, sh=sh: e.tensor_tensor(out=gview(E32, b, sh, nch), in0=gview(E32, b, sh, nch), in1=bview(b, sh, nch), op=ALU.add),
                      reads=["B%d" % b, ("E32", b)], writes=[("E32", b)])
                    A("act", lambda e, b=b, sh=sh: e.activation(out=gview(Ebf, b, sh, nch), in_=gview(E32, b, sh, nch), func=AF.Copy), reads=[("E32", b)], writes=[("Ebf", b)])
                if hooks and si in hooks:
                    hooks[si]()
            if DBG.get("scan_parts", 9) < 4 and ntok == NT:
                return
            if ntok == NT:
                A("pool", lambda e: e.tensor_copy(Sprev[:, :, 0:1], cur[:].unsqueeze(2)), reads=["carry_cur"], writes=["Sprev0"])
                A("dve", lambda e: e.tensor_copy(Sprev[:, :, 1:64], Ebf[:, :, 0:63]), reads=[("Ebf", b_) for b_ in range(4)], writes=["Sprev1"])
            A("dve", lambda e: e.tensor_copy(nxt[:].unsqueeze(2), E32[:, :, nch - 1:nch]), reads=[("E32", b_) for b_ in range(4)] + ["carry_cur"], writes=["carry_nxt"])

        def w_in_stage(ntok):
            base = cnt["win"]
            cnt["win"] += 8
            for m in range(8):
                slot = (base + m) % NWI
                wt = wring[slot]
                if (base + m) not in win_issued:
                    emit_win(base + m)
                bank = m % 4

                def f_mm(e, wt=wt, bank=bank):
                    ins = None
                    for k in range(8):
                        ins = e.matmul(B[bank][:, 0:ntok], lhsT=wt[:, k, :], rhs=nT[:, k, 0:ntok], start=(k == 0), stop=(k == 7))
                    return ins
                A("pe", f_mm, reads=["wr%d" % slot] + [("nT", k) for k in range(8)], writes=["B%d" % bank])
                if m + NWI < 8:
                    emit_win(base + m + NWI)
                if m < 4:
                    A("act", lambda e, m=m, bank=bank: e.activation(out=uT[:, m, 0:ntok], in_=B[bank][:, 0:ntok], func=AF.Copy), reads=["B%d" % bank], writes=[("uT", m)])
                else:
                    off = 16 if ntok == NT else 0
                    A("dve", lambda e, m=m, bank=bank, off=off: e.tensor_copy(vT[:, m - 4, off:off + ntok], B[bank][:, 0:ntok]), reads=["B%d" % bank], writes=[("vT", m - 4), "halo"])

        if LV >= 1:
            A("sp", lambda e: e.dma_start(out=xt[0:16, 0, :], in_=meta[:, :]), writes=[("xt", 0)], dma_key="xt0")
            rmsnorm_to_T(16, lambda s: xt[0:16, 0, :], g1col, 1, "g1col", EPS)
        if LV >= 2:
            w_in_stage(16)
        if LV >= 3:
            s5_scan(16, carry[0], carry[1], False)

        wo_order = [(br, h, c4) for br in range(2) for h in range(2) for c4 in range(4)]

        def emit_wo(base, i_):
            br, h, c4 = wo_order[i_]
            c = br * 4 + c4
            slot = (base + i_) % NWO
            wt = woring[slot]
            src = wout_t[i_]
            A("sp", lambda e, wt=wt, src=src: e.dma_start(out=wt[:, 0:512], in_=src), reads=["cv_wout"], writes=["wo%d" % slot], dma_key="wo%d" % slot)

        def emit_ffn(base, j):
            slot = (base + j) % NR
            A("sp", lambda e, slot=slot, j=j: e.dma_start(out=wgr[slot][:], in_=wg_t[j]), reads=["cv_wg"], writes=["wg%d" % slot], dma_key="wg%d" % slot)
            A("sp", lambda e, slot=slot, j=j: e.dma_start(out=wur[slot][:], in_=wu_t[j]), reads=["cv_wu"], writes=["wu%d" % slot], dma_key="wu%d" % slot)
            A("sp", lambda e, slot=slot, j=j: e.dma_start(out=wdr[slot][:], in_=wd0_t[j]), reads=["cv_wd0"], writes=["wd%d" % slot], dma_key="wd%d" % slot)

        def emit_wd1(base, j):
            slot = (base + j) % NRD1
            A("sp", lambda e, slot=slot, j=j: e.dma_start(out=wd1r[slot][:], in_=wd1_t[j]), reads=["cv_wd1"], writes=["wdd%d" % slot], dma_key="wdd%d" % slot)

        def emit_xload(tt):
            xsrc = x[tt * NT:(tt + 1) * NT, :].rearrange("(s p) d -> p s d", p=128)
            for s_ in range(4):
                A("sp", lambda e, xsrc=xsrc, s_=s_: e.dma_start(out=xt[:, s_, :], in_=xsrc[:, s_, :]), writes=[("xt", s_)], dma_key="xt%d" % s_)

        for t in range(DBG["ntiles"] if LV >= 4 else 0):
            cur, nxt = carry[(t + 1) % 2], carry[t % 2]
            P.last_w["carry_cur"] = P.last_w.get("carry_nxt")
            P.readers["carry_cur"] = []
            if t == 0:
                emit_xload(0)
            rmsnorm_to_T(128, lambda s: xt[:, s, :], g1col, 4, "g1col", EPS)
            if LV < 4.3:
                continue
            def pool_k(k, part):
                w = 2 ** (k + 1)
                v = vT[:, k, :]
                bufs = [pa, pb_, pc]

                def f_pool(e, k=k, v=v):
                    e0 = e
                    e = Serial(e0, P.chain["pool"])
                    src = v
                    ins = None
                    lo = 16 - (2 ** (k + 1) - 1)
                    sh = 1
                    lvl = 0
                    while sh < 2 ** (k + 1):
                        lo_l = lo + (2 * sh - 1)
                        dst = bufs[lvl % 3][:]
                        ins = e.tensor_tensor(out=dst[:, lo_l:16 + NT], in0=src[:, lo_l:16 + NT], in1=src[:, lo_l - sh:16 + NT - sh], op=ALU.add)
                        src = dst
                        sh *= 2
                        lvl += 1
                    return e0.nop()
                lastbuf = bufs[k % 3]
                i2 = k % 2
                yb = 4 + (k % 2)
                ykey = "B%d" % yb
                if part == "stats":
                    def f_ss2(e, k=k, i2=i2):
                        ins = None
                        for s in range(4):
                            ins = e.matmul(ST[:, 2 * s + 1:2 * s + 2], lhsT=sqb[i2][:, s * 128:(s + 1) * 128], rhs=ones[:, 0:1], start=(k == 0 and s == 0), stop=(k == 3), skip_group_check=True)
                        return ins
                    A("pe", f_ss2, reads=["sqb0", "ones"], writes=["STb"])
                    return
                if part == "mm":
                    A("pe", lambda e, k=k, yb=yb, i2=i2: e.matmul(B[yb][:], lhsT=PW[:, k, :], rhs=pbf[i2][:], start=True, stop=True), reads=["pbf0", "PW"], writes=[ykey])
                    A("dve", lambda e, k=k, yb=yb, w=w: e.tensor_scalar(out=mixT[:, 4 + k, :], in0=B[yb][:], scalar1=pscale[:, k:k + 1], scalar2=1.0 / w, op0=ALU.mult, op1=ALU.mult), reads=[ykey], writes=[("nT", 4 + k)])
                    A("act", lambda e, k=k, i2=i2: e.activation(out=sqb[i2][:], in_=mixT[:, 4 + k, :], func=AF.Square), reads=[("nT", 4 + k)], writes=["sqb0"])
                    A("dve", lambda e, k=k: e.tensor_scalar(out=mixT[:, 4 + k, :], in0=mixT[:, 4 + k, :], scalar1=gocol[:, 4 + k:5 + k], scalar2=None, op0=ALU.mult), reads=[("nT", 4 + k), "gocol"], writes=[("nT", 4 + k)])
                    return
                A("pool", f_pool, reads=[("vT", k), "halo"], writes=["pwin"])
                A("dve", lambda e, k=k, v=v, lastbuf=lastbuf, i2=i2, w=w: e.scalar_tensor_tensor(out=pbf[i2][:], in0=v[:, 16:16 + NT], scalar=-float(w), in1=lastbuf[:, 16:16 + NT], op0=ALU.mult, op1=ALU.add),
                  reads=["pwin", ("vT", k)], writes=["pbf0"])
                A("pool", lambda e, k=k: e.tensor_copy(vT[:, k, 0:16], vT[:, k, NT:NT + 16]), reads=[("vT", k)], writes=["halo"])

            w_in_stage(NT)
            A("act", lambda e: e.activation(out=ss[:, 13:14], in_=ss[:, 12:13], func=AF.Gelu_apprx_tanh), reads=["warm"], writes=["warm2"])
            if LV < 4.6:
                continue
            s5_scan(NT, cur, nxt, True, hooks={-1: lambda: pool_k(0, "front"),
                                                  0: lambda: (pool_k(0, "mm"), pool_k(1, "front")),
                                                  1: lambda: (pool_k(0, "stats"), pool_k(1, "mm"), pool_k(2, "front")),
                                                  2: lambda: (pool_k(1, "stats"), pool_k(2, "mm"), pool_k(3, "front")),
                                                  3: lambda: (pool_k(2, "stats"), pool_k(3, "mm")),
                                                  4: lambda: pool_k(3, "stats")} if LV >= 6 else None)

            if LV < 5:
                continue
            for q in range(4):
                yb = q
                ykey = "B%d" % yb

                def f_y(e, q=q, yb=yb):
                    o3 = B[yb][:].rearrange("p (c j) -> p c j", j=8)
                    u3 = uT[:, q, :].rearrange("p (c j) -> p c j", j=8)
                    e.matmul(B[yb][:], lhsT=WL[:, q, 0, :], rhs=uT[:, q, :], start=True, stop=False, skip_group_check=True)
                    for k in range(1, 8):
                        e.matmul(o3[:, :, k:8], lhsT=WL[:, q, k, :], rhs=u3[:, :, 0:8 - k], start=False, stop=False, skip_group_check=True)
                    ins = None
                    for par in range(2):
                        for j in range(8):
                            for blk in range(4):
                                o3b = B[yb][32 * blk:32 * blk + 32, :].rearrange("p (c j) -> p c j", j=8)
                                g = q * 8 + blk * 2 + par
                                pair = q * 4 + blk
                                ins = e.matmul(o3b[:, :, j], lhsT=WI[:, par, j, pair * 32:pair * 32 + 32], rhs=Sprev[:, g, :],
                                               start=False, stop=(par == 1 and j == 7), tile_position=(0, 32 * blk), skip_group_check=True)
                    return ins
                A("pe", f_y, reads=[("uT", q), "Sprev0", "Sprev1"], writes=[ykey])
            if LV < 6:
                continue
            if LV < 6.5:
                continue
            wo_base = t * 16
            ffn_base = t * NJ
            wd1_base = t * NJ
            for i_ in range(NWO):
                emit_wo(wo_base, i_)
            def emit_s5_stats(q):
                def f_ss(e, q=q):
                    ins = None
                    for s in range(4):
                        ins = e.matmul(ST[:, 2 * s:2 * s + 1], lhsT=sqb[0][:, s * 128:(s + 1) * 128], rhs=ones[:, 0:1], start=False, stop=(q == 3), skip_group_check=True)
                    return ins
                A("pe", f_ss, reads=["sqb0", "ones"], writes=["STa"])
            def emit_gelu(q):
                A("act", lambda e, q=q: e.activation(out=gbf[q % 2][:], in_=B[q][:], func=AF.Gelu_apprx_tanh), reads=["B%d" % q], writes=["gbf%d" % (q % 2)])
            emit_gelu(0)
            emit_gelu(1)
            for q in range(4):
                i2 = q % 2
                gbank = 6 if q % 2 == 0 else 5
                gk = "B%d" % gbank
                A("pe", lambda e, q=q, i2=i2, gbank=gbank: e.matmul(B[gbank][:], lhsT=GW[:, q, :], rhs=gbf[i2][:], start=True, stop=True), reads=["gbf%d" % i2], writes=[gk])
                if q > 0:
                    emit_s5_stats(q - 1)
                A("act", lambda e, q=q, i2=i2, gbank=gbank: e.activation(out=th[i2][:], in_=B[gbank][:], func=AF.Tanh, scale=0.5, bias=hb[:, q:q + 1]), reads=[gk], writes=["th%d" % i2])
                A("dve", lambda e, q=q, i2=i2: e.scalar_tensor_tensor(out=mixT[:, q, :], in0=th[i2][:], scalar=1.0, in1=gbf[i2][:], op0=ALU.add, op1=ALU.mult), reads=["th%d" % i2, "gbf%d" % i2], writes=[("nT", q)])
                if q + 2 < 4:
                    emit_gelu(q + 2)
                A("act", lambda e, q=q: e.activation(out=sqb[0][:], in_=mixT[:, q, :], func=AF.Square), reads=[("nT", q)], writes=["sqb0"])
                A("dve", lambda e, q=q: e.tensor_scalar(out=mixT[:, q, :], in0=mixT[:, q, :], scalar1=gocol[:, q:q + 1], scalar2=None, op0=ALU.mult), reads=[("nT", q), "gocol"], writes=[("nT", q)])
            emit_s5_stats(3)
            if LV < 7:
                continue
            def f_rs(e):
                st3 = ST[:, 0:8].rearrange("p (s b) -> p s b", b=2)
                r3 = rs[:, 0:8].rearrange("p (s b) -> p s b", b=2)
                e.tensor_scalar(out=r3[:, :, 0], in0=st3[:, :, 0], scalar1=1.0 / 512, scalar2=4.0 * EPS, op0=ALU.mult, op1=ALU.add)
                return e.tensor_scalar(out=r3[:, :, 1], in0=st3[:, :, 1], scalar1=1.0 / 512, scalar2=EPS, op0=ALU.mult, op1=ALU.add)
            A("dve", f_rs, reads=["STa", "STb"], writes=["rs"])
            A("act", lambda e: e.activation(out=ss[:, 0:8], in_=rs[:, 0:8], func=AF.Sqrt), reads=["rs"], writes=[("ss", s_) for s_ in range(4)] + ["ss8"])
            A("dve", lambda e: e.reciprocal(out=rs[:, 0:8], in_=ss[:, 0:8]), reads=[("ss", s_) for s_ in range(4)] + ["ss8"], writes=["rs"])

            wo_list = [(br, h, c4) for br in range(2) for h in range(2) for c4 in range(4)]
            for i_, (br, h, c4) in enumerate(wo_list):
                c = br * 4 + c4
                slot = (wo_base + i_) % NWO
                wt = woring[slot]

                for s in range(4):
                    A("pe", lambda e, wt=wt, c=c, c4=c4, s=s: e.matmul(B[s][:], lhsT=mixT[:, c, s * 128:(s + 1) * 128], rhs=wt[:, 0:512], start=(c4 == 0), stop=(c4 == 3)),
                      reads=["wo%d" % slot, ("nT", c)], writes=["B%d" % s])
                if i_ + NWO < len(wo_list):
                    emit_wo(wo_base, i_ + NWO)
                if c4 == 3:
                    for s in range(4):
                        A("dve", lambda e, s=s, h=h, br=br: e.scalar_tensor_tensor(out=xt[:, s, h * 512:(h + 1) * 512], in0=B[s][:], scalar=rs[:, 2 * s + br:2 * s + br + 1], in1=xt[:, s, h * 512:(h + 1) * 512], op0=ALU.mult, op1=ALU.add),
                          reads=["B%d" % s, "rs", ("xt", s)], writes=[("xt", s)])
            if LV < 8:
                continue
            for j in range(NR):
                emit_ffn(ffn_base, j)
            for j in range(NRD1):
                emit_wd1(wd1_base, j)
            if t + 1 < DBG["ntiles"]:
                for m in range(NWI):
                    emit_win(cnt["win"] + m)
            rmsnorm_to_T(128, lambda s: xt[:, s, :], g2col, 4, "g2col", EPS)
            pend = []

            def emit_down(slot, j):
                def f_d(e, slot=slot, j=j):
                    ins = None
                    for s in range(4):
                        ins = e.matmul(B[s][:], lhsT=ffT[:, j, s * 128:(s + 1) * 128], rhs=wdr[slot][:], start=(j == 0), stop=(j == NJ - 1))
                    return ins
                A("pe", f_d, reads=["wd%d" % slot, ("ffT", j)], writes=["B0", "B1", "B2", "B3"])
                if j + NR < NJ:
                    emit_ffn(ffn_base, j + NR)
            for j in range(NJ):
                slot = (ffn_base + j) % NR
                gb = 4 if j % 2 == 0 else 6
                gkey = "B%d" % gb

                def f_g(e, slot=slot, gb=gb):
                    ins = None
                    for k in range(8):
                        ins = e.matmul(B[gb][:], lhsT=wgr[slot][:, k, :], rhs=nT[:, k, :], start=(k == 0), stop=(k == 7))
                    return ins
                A("pe", f_g, reads=["wg%d" % slot] + [("nT", k) for k in range(8)], writes=[gkey])

                def f_u(e, slot=slot):
                    ins = None
                    for k in range(8):
                        ins = e.matmul(B[5][:], lhsT=wur[slot][:, k, :], rhs=nT[:, k, :], start=(k == 0), stop=(k == 7))
                    return ins
                A("pe", f_u, reads=["wu%d" % slot] + [("nT", k) for k in range(8)], writes=["B5"])
                i2 = j % 2
                A("act", lambda e, gb=gb, i2=i2: e.activation(out=sil[i2][:], in_=B[gb][:], func=AF.Silu), reads=[gkey], writes=["g2%d" % i2])
                A("dve", lambda e, j=j, i2=i2: e.tensor_tensor(out=ffT[:, j, :], in0=sil[i2][:], in1=B[5][:], op=ALU.mult), reads=["g2%d" % i2, "B5"], writes=[("ffT", j)])

                pend.append((slot, j))
                if len(pend) > 1:
                    emit_down(*pend.pop(0))
            while pend:
                emit_down(*pend.pop(0))
            for s in range(4):
                A("dve", lambda e, s=s: e.tensor_tensor(out=xt[:, s, 0:512], in0=xt[:, s, 0:512], in1=B[s][:], op=ALU.add), reads=["B%d" % s, ("xt", s)], writes=[("xt", s)])
            A("act", lambda e: e.activation(out=ss[:, 13:14], in_=ss[:, 12:13], func=AF.Sqrt), reads=["warm"], writes=["warm2"])
            for j in range(NJ):
                slot = (wd1_base + j) % NRD1

                def f_d1(e, slot=slot, j=j):
                    ins = None
                    for s in range(4):
                        ins = e.matmul(B[s][:], lhsT=ffT[:, j, s * 128:(s + 1) * 128], rhs=wd1r[slot][:], start=(j == 0), stop=(j == NJ - 1))
                    return ins
                A("pe", f_d1, reads=["wdd%d" % slot, ("ffT", j)], writes=["B0", "B1", "B2", "B3"])
                if j + NRD1 < NJ:
                    emit_wd1(wd1_base, j + NRD1)
            for s in range(4):
                A("dve", lambda e, s=s: e.tensor_tensor(out=xt[:, s, 512:1024], in0=xt[:, s, 512:1024], in1=B[s][:], op=ALU.add), reads=["B%d" % s, ("xt", s)], writes=[("xt", s)])
            if LV < 9:
                continue
            for s in range(4):
                A("act", lambda e, s=s: e.activation(out=junk[:], in_=xt[:, s, :], func=AF.Square, accum_out=ss[:, s:s + 1]), reads=[("xt", s)], writes=[("ss", s), "nbf0"])
            A("dve", lambda e: e.tensor_scalar(out=rs[:, 0:4], in0=ss[:, 0:4], scalar1=1.0 / D, scalar2=EPS, op0=ALU.mult, op1=ALU.add), reads=[("ss", s_) for s_ in range(4)], writes=["rs"])
            A("act", lambda e: e.activation(out=ss[:, 0:4], in_=rs[:, 0:4], func=AF.Sqrt), reads=["rs"], writes=[("ss", s_) for s_ in range(4)])
            A("dve", lambda e: e.reciprocal(out=rs[:, 0:4], in_=ss[:, 0:4]), reads=[("ss", s_) for s_ in range(4)], writes=["rs"])
            for s in range(4):
                A("dve", lambda e, s=s: e.scalar_tensor_tensor(out=ost[:, s, :], in0=xt[:, s, :], scalar=rs[:, s:s + 1], in1=fgb[:], op0=ALU.mult, op1=ALU.mult), reads=["rs", ("xt", s), "fgb"], writes=UNIK)
            if t + 1 < DBG["ntiles"]:
                emit_xload(t + 1)
            ydst = y[t * NT:(t + 1) * NT, :].rearrange("(s p) d -> p s d", p=128)
            A("sp", lambda e, ydst=ydst: e.dma_start(out=ydst, in_=ost), reads=UNIK, dma_key="yout")

        dumpable = {
            "WV": WV[:].rearrange("p a q k c -> p (a q k c)"), "WI": WI[:].rearrange("p a j c -> p (a j c)"),
            "WL": WL[:].rearrange("p q k c -> p (q k c)"), "AA": AAm[:, 0:4].rearrange("p g k c -> p (g k c)"),
            "GW": GW[:].rearrange("p q c -> p (q c)"), "hb": hb[:], "carry0": carry[0][:], "carry1": carry[1][:],
            "nT": nT[:].rearrange("p k n -> p (k n)"), "uT": uT[:].rearrange("p k n -> p (k n)"),
            "vT": vT[:].rearrange("p k n -> p (k n)"), "E32": E32.rearrange("p g c -> p (g c)"),
            "Sprev": Sprev[:].rearrange("p g c -> p (g c)"), "xt": xt[:].rearrange("p s d -> p (s d)"), "rs": rs[:],
        }
        fw = []
        for i, name in enumerate(DBG["dumps"]):
            ap = dumpable[name]
            dd = nc.dram_tensor("dump_" + name, [128, ap.shape[1]], F32, kind="ExternalOutput").ap()
            allk = list(set(P.last_w))
            A("pool", lambda e, dd=dd, ap=ap: e.dma_start(out=dd[:, :], in_=ap), reads=allk, dma_key="dump%d" % i)
            fw.append("dump%d" % i)
        fw += DBG.get("sd_keys", [])
        if any(o["dma_key"] == "yout" for o in P.ops):
            fw.append("yout")
        print('sbuf bytes remaining', nc.sbuf_bytes_remaining)
        P.emit(nc, final_wait_keys=fw)
    return nc


_NC_CACHE = {}


def kernel(**inputs):
    if "nc" not in _NC_CACHE:
        _NC_CACHE["nc"] = build_nc()
    nc = _NC_CACHE["nc"]
    f = lambda a: np.ascontiguousarray(np.asarray(a, dtype=np.float32))
    shared = {
        "meta_tokens": f(inputs["meta_tokens"]),
        "cst": make_consts(),
        "norm1_g": f(inputs["norm1_g"]).reshape(D),
        "w_in": f(inputs["w_in"]).reshape(D, D),
        "ssm_lambda_re": f(inputs["ssm_lambda_re"]).reshape(32, 64),
        "ssm_lambda_im": f(inputs["ssm_lambda_im"]).reshape(32, 64),
        "ssm_log_step": f(inputs["ssm_log_step"]).reshape(32),
        "ssm_b_re": f(inputs["ssm_b_re"]).reshape(32, 64, 16),
        "ssm_b_im": f(inputs["ssm_b_im"]).reshape(32, 64, 16),
        "ssm_c_re": f(inputs["ssm_c_re"]).reshape(32, 16, 64),
        "ssm_c_im": f(inputs["ssm_c_im"]).reshape(32, 16, 64),
        "ssm_d": f(inputs["ssm_d"]).reshape(512),
        "ssm_glu_w": f(inputs["ssm_glu_w"]).reshape(32, 16, 16),
        "ssm_glu_b": f(inputs["ssm_glu_b"]).reshape(512),
        "ssm_norm_g": f(inputs["ssm_norm_g"]).reshape(512),
        "pool_w": f(inputs["pool_w"]).reshape(4, 128, 128),
        "pool_scale": f(inputs["pool_scale"]).reshape(512),
        "pool_norm_g": f(inputs["pool_norm_g"]).reshape(512),
        "w_out": f(inputs["w_out"]).reshape(D, D),
        "norm2_g": f(inputs["norm2_g"]).reshape(D),
        "w_gate": f(inputs["w_gate"]).reshape(D, DFF),
        "w_up": f(inputs["w_up"]).reshape(D, DFF),
        "w_down": f(inputs["w_down"]).reshape(DFF, D),
        "final_norm_g": f(inputs["final_norm_g"]).reshape(D),
    }
    xs = f(inputs["x"])
    in_maps = [dict(shared, x=xs[b]) for b in range(8)]
    res = run_bass_kernel_spmd(nc, in_maps, core_ids=list(range(8)))
    return np.stack([r["y"] for r in res.results], axis=0).astype(np.float32)
```

```python
import math
from contextlib import ExitStack

import numpy as np
import concourse.bass as bass
import concourse.mybir as mybir
from concourse.bass_utils import run_bass_kernel_spmd

F32 = mybir.dt.float32
BF16 = mybir.dt.bfloat16
ALU = mybir.AluOpType
AF = mybir.ActivationFunctionType

D = 1024
SEQ = 4096
NMETA = 16
DFF = 2816
NJ = DFF // 128
NT = 512
NTILES = SEQ // NT
EPS = 1e-6
TWO_PI = 2.0 * math.pi
C1 = 6.28125
C2 = TWO_PI - C1


class Prog:
    COMPUTE = ("pe", "act", "dve", "pool")
    ALL = ("pe", "act", "dve", "pool", "sp")

    def __init__(self):
        self.ops = []
        self.last_w = {}
        self.readers = {}
        self.eng_pos = {e: 0 for e in self.ALL}
        self.chain = {}

    def add(self, eng, fn, reads=(), writes=(), dma_key=None, ndma=1):
        idx = len(self.ops)
        deps = set()
        for r in reads:
            if r in self.last_w:
                deps.add(self.last_w[r])
        for w in writes:
            if w in self.last_w:
                deps.add(self.last_w[w])
            for rd in self.readers.get(w, ()):
                deps.add(rd)
        pos = self.eng_pos[eng]
        self.eng_pos[eng] += 1
        keep = set()
        for d in deps:
            o = self.ops[d]
            if o["dma_key"] is None and o["eng"] == eng and dma_key is None:
                if eng == "pe":
                    continue
            keep.add(d)
        self.ops.append(dict(eng=eng, fn=fn, deps=keep, dma_key=dma_key, idx=idx, pos=pos,
                             signal=dma_key is not None, sigval=None, ndma=ndma))
        for r in reads:
            self.readers.setdefault(r, []).append(idx)
        for w in writes:
            self.last_w[w] = idx
            self.readers[w] = []
        return idx

    def barrier(self):
        keys = list(set(self.last_w) | set(self.readers))
        for e in self.ALL:
            self.add(e, lambda eng: eng.nop(), writes=keys)

    def emit(self, nc, final_wait_keys=()):
        ops = self.ops
        for o in ops:
            for d in o["deps"]:
                ops[d]["signal"] = True
        cnt = {e: 0 for e in self.ALL}
        dcnt = {}
        dma_keys = []
        for o in ops:
            if o["dma_key"] is not None:
                k = o["dma_key"]
                if k not in dcnt:
                    dcnt[k] = 0
                    dma_keys.append(k)
                dcnt[k] += 16 * o["ndma"]
                o["sigval"] = dcnt[k]
            elif o["signal"]:
                cnt[o["eng"]] += 1
                o["sigval"] = cnt[o["eng"]]
        with ExitStack() as st:
            sems = {}
            for e in self.ALL:
                sems[("eng", e)] = st.enter_context(nc.semaphore("s_" + e))
            for i, k in enumerate(dma_keys):
                sems[("dma", k)] = st.enter_context(nc.semaphore("d_%d" % i))
            for e in ("dve", "pool", "act"):
                self.chain[e] = [st.enter_context(nc.semaphore("c_" + e)), 0]
            block = st.enter_context(nc.Block())
            per_eng = {e: [o for o in ops if o["eng"] == e] for e in self.ALL}

            def semof(o):
                if o["dma_key"] is not None:
                    return sems[("dma", o["dma_key"])]
                return sems[("eng", o["eng"])]

            def body(engname):
                def f(eng):
                    known = {}
                    for o in per_eng[engname]:
                        need = {}
                        for d in o["deps"]:
                            od = ops[d]
                            s = semof(od)
                            v = od["sigval"]
                            if known.get(s.num, 0) >= v:
                                continue
                            if need.get(s.num, (None, 0))[1] < v:
                                need[s.num] = (s, v)
                        for s, v in need.values():
                            eng.wait_ge(s, v)
                            known[s.num] = v
                        ins = o["fn"](eng)
                        if o["signal"]:
                            if o["dma_key"] is not None:
                                il = ins if isinstance(ins, (list, tuple)) else [ins]
                                assert len(il) == o["ndma"]
                                for i_ in il:
                                    i_.then_inc(semof(o), 16)
                            else:
                                ins.then_inc(semof(o), 1)
                    if engname == "sp":
                        for k in final_wait_keys:
                            eng.wait_ge(sems[("dma", k)], dcnt[k])
                return f

            block.tensor(body("pe"))
            block.scalar(body("act"))
            block.vector(body("dve"))
            block.gpsimd(body("pool"))
            block.sync(body("sp"))


class Serial:
    def __init__(self, e, chain):
        self._e = e
        self._c = chain

    def __getattr__(self, name):
        fn = getattr(self._e, name)

        def w(*a, **k):
            ins = fn(*a, **k)
            self._c[1] += 1
            ins.then_inc(self._c[0], 1)
            self._e.wait_ge(self._c[0], self._c[1])
            return ins
        return w


class Spaced:
    def __init__(self, e, dummy):
        self._e = e
        self._d = dummy

    def __getattr__(self, name):
        fn = getattr(self._e, name)

        def w(*a, **k):
            ins = fn(*a, **k)
            self._e.memset(self._d, 0.0)
            return ins
        return w


CO_ID, CO_BD, CO_JM, CO_RME, CO_RMO, CO_PH, CO_SGN, CO_CME, CO_CMO = 0, 128, 256, 320, 321, 322, 323, 324, 836
CW = 1348


def make_consts():
    c = np.zeros((128, CW), np.float32)
    p = np.arange(128)
    c[:, CO_ID:CO_ID + 128] = np.eye(128)
    c[:, CO_BD:CO_BD + 128] = (p[:, None] // 16 == p[None, :] // 16)
    c[:, CO_JM:CO_JM + 64] = (p[:, None] % 64 == np.arange(64)[None, :])
    c[:, CO_RME] = ((p // 16) % 2 == 0)
    c[:, CO_RMO] = ((p // 16) % 2 == 1)
    c[:, CO_PH] = np.where(p < 64, math.pi / 2, 0.0)
    c[:, CO_SGN] = np.where(p < 64, 1.0, -1.0)
    col = np.arange(512)
    c[:, CO_CME:CO_CME + 512] = ((col // 16) % 2 == 0)[None, :]
    c[:, CO_CMO:CO_CMO + 512] = ((col // 16) % 2 == 1)[None, :]
    return c


DBG = {"level": 99, "ntiles": NTILES, "dumps": []}


def build_nc():
    LV = DBG["level"]
    nc = bass.Bass("TRN2", target_bir_lowering=False)

    def din(name, shape):
        return nc.dram_tensor(name, shape, F32, kind="ExternalInput").ap()

    x = din("x", [SEQ, D])
    meta = din("meta_tokens", [NMETA, D])
    cst_d = din("cst", [128, CW])
    norm1_g = din("norm1_g", [D])
    w_in = din("w_in", [D, D])
    lam_re = din("ssm_lambda_re", [32, 64])
    lam_im = din("ssm_lambda_im", [32, 64])
    log_step = din("ssm_log_step", [32])
    b_re = din("ssm_b_re", [32, 64, 16])
    b_im = din("ssm_b_im", [32, 64, 16])
    c_re = din("ssm_c_re", [32, 16, 64])
    c_im = din("ssm_c_im", [32, 16, 64])
    ssm_d = din("ssm_d", [512])
    glu_w = din("ssm_glu_w", [32, 16, 16])
    glu_b = din("ssm_glu_b", [512])
    ssm_ng = din("ssm_norm_g", [512])
    pool_w = din("pool_w", [4, 128, 128])
    pool_scale = din("pool_scale", [512])
    pool_ng = din("pool_norm_g", [512])
    w_out = din("w_out", [D, D])
    norm2_g = din("norm2_g", [D])
    w_gate = din("w_gate", [D, DFF])
    w_up = din("w_up", [D, DFF])
    w_down = din("w_down", [DFF, D])
    final_g = din("final_norm_g", [D])
    y = nc.dram_tensor("y", [SEQ, D], F32, kind="ExternalOutput").ap()
    win_t = nc.dram_tensor("win_t", [8, 128, 8, 128], BF16).ap()
    wout_t = nc.dram_tensor("wout_t", [16, 128, 512], BF16).ap()
    wg_t = nc.dram_tensor("wg_t", [NJ, 128, 8, 128], BF16).ap()
    wu_t = nc.dram_tensor("wu_t", [NJ, 128, 8, 128], BF16).ap()
    wd0_t = nc.dram_tensor("wd0_t", [NJ, 128, 512], BF16).ap()
    wd1_t = nc.dram_tensor("wd1_t", [NJ, 128, 512], BF16).ap()

    P = Prog()
    A = P.add

    with ExitStack() as st:
        def sb(name, shape, dt=F32):
            return st.enter_context(nc.sbuf_tensor(name, shape, dt))

        def psb(name):
            return st.enter_context(nc.psum_tensor(name, [128, 512], F32))

        B = [psb("B%d" % i) for i in range(8)]
        TP = B[7][:, 0:256].bitcast(BF16)
        TP2 = B[6][:, 0:256].bitcast(BF16)
        ST = B[7][:, 256:272]

        identb = sb("identb", [128, 128], BF16)
        WV = sb("WV", [128, 2, 4, 8, 128], BF16)
        WI = sb("WI", [128, 2, 8, 512], BF16)
        WL = sb("WL", [128, 4, 8, 128], BF16)
        AAm = sb("AAm", [128, 32, 6, 128], BF16)
        GW = sb("GW", [128, 4, 128], BF16)
        PW = sb("PW", [128, 4, 128], BF16)
        hb = sb("hb_sb", [128, 4])
        g1col = sb("g1col", [128, 8])
        g2col = sb("g2col", [128, 8])
        gocol = sb("gocol", [128, 8])
        pscale = sb("pscale", [128, 4])
        fgb = sb("fgb", [128, D])
        carry = [sb("carry0", [128, 32], BF16), sb("carry1", [128, 32], BF16)]
        vT = sb("vT", [128, 4, 16 + NT])

        with ExitStack() as s2:
            def tb(name, shape, dt=F32):
                return s2.enter_context(nc.sbuf_tensor(name, shape, dt))

            cst = tb("cst_sb", [128, CW])
            ident = cst[:, CO_ID:CO_ID + 128]
            bd = cst[:, CO_BD:CO_BD + 128]
            jm = cst[:, CO_JM:CO_JM + 64]
            A("sp", lambda e: e.dma_start(out=cst[:], in_=cst_d[:, :]), writes=["cst"], dma_key="cst")
            A("dve", lambda e: e.tensor_copy(identb[:], ident), reads=["cst"], writes=["identb"])

            def small_load(name, dst, src, n=1):
                def f(e):
                    with nc.allow_non_contiguous_dma(reason="tiny param"):
                        return e.dma_start(out=dst, in_=src)
                A("sp", f, writes=[name], dma_key=name)

            lre = tb("lre", [128, 32]); lim = tb("lim", [128, 32]); lst = tb("lst", [128, 32])
            small_load("lst", lst[:], log_step.partition_broadcast(128))
            bre = tb("bre", [128, 32, 16]); bim = tb("bim", [128, 32, 16])
            cre = tb("cre", [128, 32, 16]); cim = tb("cim", [128, 32, 16])
            BREK = ["bre%d_%d" % (h_, g_) for h_ in range(2) for g_ in range(4)]
            BIMK = ["bim%d_%d" % (h_, g_) for h_ in range(2) for g_ in range(4)]

            tl_n = [0]

            def tload(src2d, R, dst, key):
                i = tl_n[0]
                tl_n[0] += 1
                nat = tb("nat%d" % i, [128, 128])
                A("sp", lambda e: [e.dma_start(out=nat[0:R, 0:64], in_=src2d), e.dma_start(out=nat[0:R, 64:128], in_=src2d)], writes=["nat%d" % i], dma_key="nat%d" % i, ndma=2)
                bank = i % 4
                A("pe", lambda e: e.transpose(B[bank][:, 0:R], nat[0:R, :], ident[0:R, 0:R]), reads=["nat%d" % i, "cst"], writes=["B%d" % bank])
                A("act", lambda e: e.activation(out=dst, in_=B[bank][:, 0:R], func=AF.Copy), reads=["B%d" % bank], writes=[key])
            tload(lam_re[:, :], 32, lre[:], "lre")
            tload(lam_im[:, :], 32, lim[:], "lim")
            c2r = c_re.rearrange("g h p -> (g h) p")
            c2i = c_im.rearrange("g h p -> (g h) p")
            for q in range(4):
                tload(c2r[q * 128:(q + 1) * 128, :], 128, cre[:, q * 8:(q + 1) * 8, :].rearrange("p g h -> p (g h)"), "cre%d" % q)
                tload(c2i[q * 128:(q + 1) * 128, :], 128, cim[:, q * 8:(q + 1) * 8, :].rearrange("p g h -> p (g h)"), "cim%d" % q)
            CREK = ["cre%d" % q for q in range(4)]
            CIMK = ["cim%d" % q for q in range(4)]
            for hlf in range(2):
                for gq in range(4):
                    sl = slice(64 * hlf, 64 * hlf + 64)
                    gs = slice(8 * gq, 8 * gq + 8)
                    small_load("bre%d" % hlf + "_%d" % gq, bre[sl, gs, :], b_re[gs].rearrange("g p h -> p g h"))
                    small_load("bim%d" % hlf + "_%d" % gq, bim[sl, gs, :], b_im[gs].rearrange("g p h -> p g h"))
            def conv(key, pairs):
                def f(e):
                    return [e.dma_start(out=d_, in_=s_) for d_, s_ in pairs]
                A("pool", f, reads=["lre", "lim", "lst"] + BREK + BIMK + CREK + CIMK, writes=[key], dma_key=key, ndma=len(pairs))
            conv("cv_win", [(win_t[m], w_in[:, m * 128:(m + 1) * 128].rearrange("(k p) c -> p k c", p=128)) for m in range(8)])
            wo_order0 = [(br, h, c4) for br in range(2) for h in range(2) for c4 in range(4)]
            conv("cv_wout", [(wout_t[i_], w_out[(br * 4 + c4) * 128:(br * 4 + c4 + 1) * 128, h * 512:(h + 1) * 512]) for i_, (br, h, c4) in enumerate(wo_order0)])
            conv("cv_wg", [(wg_t[j], w_gate[:, j * 128:(j + 1) * 128].rearrange("(k p) c -> p k c", p=128)) for j in range(NJ)])
            conv("cv_wu", [(wu_t[j], w_up[:, j * 128:(j + 1) * 128].rearrange("(k p) c -> p k c", p=128)) for j in range(NJ)])
            conv("cv_wd0", [(wd0_t[j], w_down[j * 128:(j + 1) * 128, 0:512]) for j in range(NJ)])
            conv("cv_wd1", [(wd1_t[j], w_down[j * 128:(j + 1) * 128, 512:1024]) for j in range(NJ)])

            LK = ["lre", "lim", "lst"]
            dcol = tb("dcol", [128, 4])
            small_load("dcol", dcol[:], ssm_d.rearrange("(q p) -> p q", p=128))
            small_load("hb", hb[:], glu_b.rearrange("(q p) -> p q", p=128))
            small_load("g1col", g1col[:], norm1_g.rearrange("(k p) -> p k", p=128))
            small_load("g2col", g2col[:], norm2_g.rearrange("(k p) -> p k", p=128))
            small_load("gocolA", gocol[:, 0:4], ssm_ng.rearrange("(k p) -> p k", p=128))
            small_load("gocolB", gocol[:, 4:8], pool_ng.rearrange("(k p) -> p k", p=128))
            small_load("pscale", pscale[:], pool_scale.rearrange("(k p) -> p k", p=128))
            small_load("fgb", fgb[:], final_g.partition_broadcast(128))
            gwr = tb("gwr", [128, 4, 8, 16])
            for q in range(4):
                src = bass.AP(glu_w.tensor, q * 2048, [[16, 128], [0, 8], [1, 16]])
                small_load("gwr%d" % q, gwr[:, q], src)
            A("pool", lambda e: e.dma_start(out=PW[:], in_=pool_w.rearrange("k c d -> c k d")), writes=["PW"], dma_key="PW")
            T = {n: tb("t_" + n, [128, 32]) for n in ("step", "lr", "z", "mag", "ang", "r", "m", "r2", "sn", "cs", "Ar", "Ai", "den", "nr", "cr", "ci", "u1", "u2", "u3")}

            dum = tb("dum", [128, 64])

            def dv(fn, reads, writes):
                def g(e):
                    fn(Serial(e, P.chain["dve"]))
                    return e.nop()
                A("dve", g, reads=reads, writes=writes)

            def exp_taylor(dst, src, nsq, deg, key):
                u = T["u1"]
                fact = [1.0]
                for k in range(1, deg + 1):
                    fact.append(fact[-1] * k)

                def f(e):
                    e.tensor_scalar(out=u[:], in0=src[:], scalar1=1.0 / (2 ** nsq), scalar2=None, op0=ALU.mult)
                    cur_, oth_ = T["u2"], dst
                    e.tensor_scalar(out=cur_[:], in0=u[:], scalar1=1.0 / fact[deg], scalar2=None, op0=ALU.mult)
                    for k in range(deg - 1, 0, -1):
                        e.scalar_tensor_tensor(out=oth_[:], in0=cur_[:], scalar=1.0 / fact[k], in1=u[:], op0=ALU.add, op1=ALU.mult)
                        cur_, oth_ = oth_, cur_
                    e.tensor_scalar(out=oth_[:], in0=cur_[:], scalar1=1.0, scalar2=None, op0=ALU.add)
                    cur_, oth_ = oth_, cur_
                    for _ in range(nsq):
                        e.tensor_tensor(out=oth_[:], in0=cur_[:], in1=cur_[:], op=ALU.mult)
                        cur_, oth_ = oth_, cur_
                    if cur_ is not dst:
                        e.tensor_copy(dst[:], cur_[:])
                    return None
                return f
            dv(exp_taylor(T["step"], lst, 3, 12, "step"), LK, ["step"])

            def f_lr(e):
                e.tensor_scalar(out=T["lr"][:], in0=lre[:], scalar1=-1e-4, scalar2=None, op0=ALU.min)
                e.tensor_tensor(out=T["ang"][:], in0=lim[:], in1=T["step"][:], op=ALU.mult)
                return e.tensor_tensor(out=T["z"][:], in0=T["lr"][:], in1=T["step"][:], op=ALU.mult)
            dv(f_lr, LK + ["step"], ["lr", "z", "ang"])
            dv(exp_taylor(T["mag"], T["z"], 0, 10, "mag"), ["z"], ["mag"])

            def f_rr(e):
                e.tensor_scalar(out=T["r"][:], in0=T["ang"][:], scalar1=cst[:, CO_PH:CO_PH + 1], scalar2=None, op0=ALU.add)
                for it in range(6):
                    e.tensor_scalar(out=T["m"][:], in0=T["r"][:], scalar1=math.pi, scalar2=None, op0=ALU.is_gt)
                    e.scalar_tensor_tensor(out=T["r2"][:], in0=T["m"][:], scalar=-C1, in1=T["r"][:], op0=ALU.mult, op1=ALU.add)
                    e.scalar_tensor_tensor(out=T["r"][:], in0=T["m"][:], scalar=-C2, in1=T["r2"][:], op0=ALU.mult, op1=ALU.add)
                for it in range(1):
                    e.tensor_scalar(out=T["m"][:], in0=T["r"][:], scalar1=-math.pi, scalar2=None, op0=ALU.is_lt)
                    e.scalar_tensor_tensor(out=T["r2"][:], in0=T["m"][:], scalar=C1, in1=T["r"][:], op0=ALU.mult, op1=ALU.add)
                    e.scalar_tensor_tensor(out=T["r"][:], in0=T["m"][:], scalar=C2, in1=T["r2"][:], op0=ALU.mult, op1=ALU.add)
                return None
            dv(f_rr, ["ang", "cst"], ["r"])
            A("act", lambda e: e.activation(out=T["sn"][:], in_=T["r"][:], func=AF.Sin), reads=["r"], writes=["sn"])
            def f_rr2(e):
                e.tensor_scalar(out=T["r2"][:], in0=cst[:, CO_PH:CO_PH + 1].to_broadcast([128, 32]), scalar1=-2.0, scalar2=math.pi / 2, op0=ALU.mult, op1=ALU.add)
                e.tensor_tensor(out=T["u1"][:], in0=T["r"][:], in1=T["r2"][:], op=ALU.add)
                e.tensor_scalar(out=T["m"][:], in0=T["u1"][:], scalar1=math.pi, scalar2=None, op0=ALU.is_gt)
                e.scalar_tensor_tensor(out=T["u2"][:], in0=T["m"][:], scalar=-C1, in1=T["u1"][:], op0=ALU.mult, op1=ALU.add)
                e.tensor_copy(T["u3"][:], T["m"][:])
                e.scalar_tensor_tensor(out=T["u1"][:], in0=T["u3"][:], scalar=-C2, in1=T["u2"][:], op0=ALU.mult, op1=ALU.add)
                e.tensor_scalar(out=T["m"][:], in0=T["u1"][:], scalar1=-math.pi, scalar2=None, op0=ALU.is_lt)
                e.scalar_tensor_tensor(out=T["u2"][:], in0=T["m"][:], scalar=C1, in1=T["u1"][:], op0=ALU.mult, op1=ALU.add)
                e.tensor_copy(T["u3"][:], T["m"][:])
                return e.scalar_tensor_tensor(out=T["r2"][:], in0=T["u3"][:], scalar=C2, in1=T["u2"][:], op0=ALU.mult, op1=ALU.add)
            dv(f_rr2, ["r", "cst"], ["r2"])
            A("act", lambda e: e.activation(out=T["cs"][:], in_=T["r2"][:], func=AF.Sin), reads=["r2"], writes=["cs"])
            def f_a(e):
                e.tensor_tensor(out=T["Ar"][0:64, :], in0=T["mag"][0:64, :], in1=T["sn"][0:64, :], op=ALU.mult)
                e.tensor_tensor(out=T["Ai"][0:64, :], in0=T["mag"][0:64, :], in1=T["cs"][0:64, :], op=ALU.mult)
                e.tensor_tensor(out=T["Ar"][64:128, :], in0=T["mag"][64:128, :], in1=T["cs"][64:128, :], op=ALU.mult)
                e.tensor_tensor(out=T["Ai"][64:128, :], in0=T["mag"][64:128, :], in1=T["sn"][64:128, :], op=ALU.mult)
                e.tensor_tensor(out=T["u1"][:], in0=T["lr"][:], in1=T["lr"][:], op=ALU.mult)
                e.tensor_tensor(out=T["u2"][:], in0=lim[:], in1=lim[:], op=ALU.mult)
                e.tensor_scalar(out=T["nr"][:], in0=T["Ar"][:], scalar1=-1.0, scalar2=None, op0=ALU.add)
                e.tensor_tensor(out=T["den"][:], in0=T["u1"][:], in1=T["u2"][:], op=ALU.add)
                e.tensor_tensor(out=T["u1"][:], in0=T["nr"][:], in1=T["lr"][:], op=ALU.mult)
                e.reciprocal(out=T["u3"][:], in_=T["den"][:])
                e.tensor_tensor(out=T["u2"][:], in0=T["Ai"][:], in1=lim[:], op=ALU.mult)
                e.tensor_tensor(out=T["cr"][:], in0=T["u1"][:], in1=T["u2"][:], op=ALU.add)
                e.tensor_tensor(out=T["u1"][:], in0=T["Ai"][:], in1=T["lr"][:], op=ALU.mult)
                e.tensor_tensor(out=T["u2"][:], in0=T["nr"][:], in1=lim[:], op=ALU.mult)
                e.tensor_tensor(out=T["ci"][:], in0=T["u1"][:], in1=T["u2"][:], op=ALU.subtract)
                e.tensor_tensor(out=T["den"][:], in0=T["cr"][:], in1=T["u3"][:], op=ALU.mult)
                e.tensor_tensor(out=T["nr"][:], in0=T["ci"][:], in1=T["u3"][:], op=ALU.mult)
                e.tensor_copy(T["cr"][:], T["den"][:])
                return e.tensor_copy(T["ci"][:], T["nr"][:])
            dv(f_a, ["mag", "sn", "cs", "lr"] + LK, ["a", "zoh"])

            Pr = tb("Pr", [128, 9, 32]); Pi = tb("Pi", [128, 9, 32])
            Qr = tb("Qr", [128, 6, 32]); Qi = tb("Qi", [128, 6, 32])
            w1 = tb("w1", [128, 4, 32]); w2 = tb("w2", [128, 4, 32])

            def cmul(e, outr, outi, ar, ai, br, bi, shape):
                s1 = w1[:, 0:shape] if shape else w1[:, 0]
                s2_ = w2[:, 0:shape] if shape else w2[:, 0]
                e.tensor_tensor(out=s1, in0=ar, in1=br, op=ALU.mult)
                e.tensor_tensor(out=s2_, in0=ai, in1=bi, op=ALU.mult)
                e.tensor_tensor(out=outr, in0=s1, in1=s2_, op=ALU.subtract)
                e.tensor_tensor(out=s1, in0=ar, in1=bi, op=ALU.mult)
                e.tensor_tensor(out=s2_, in0=ai, in1=br, op=ALU.mult)
                return e.tensor_tensor(out=outi, in0=s1, in1=s2_, op=ALU.add)

            def f_pow(e):
                e.memset(Pr[:, 0, :], 1.0)
                e.memset(Pi[:, 0, :], 0.0)
                e.tensor_copy(Pr[:, 1, :], T["Ar"][:])
                e.tensor_copy(Pi[:, 1, :], T["Ai"][:])
                cmul(e, Pr[:, 2, :], Pi[:, 2, :], Pr[:, 1, :], Pi[:, 1, :], Pr[:, 1, :], Pi[:, 1, :], 0)
                bc = lambda t, k, n: t[:, k:k + 1, :].to_broadcast([128, n, 32])
                cmul(e, Pr[:, 3:5, :], Pi[:, 3:5, :], Pr[:, 1:3, :], Pi[:, 1:3, :], bc(Pr, 2, 2), bc(Pi, 2, 2), 2)
                cmul(e, Pr[:, 5:9, :], Pi[:, 5:9, :], Pr[:, 1:5, :], Pi[:, 1:5, :], bc(Pr, 4, 4), bc(Pi, 4, 4), 4)
                e.tensor_copy(Qr[:, 0, :], Pr[:, 8, :])
                ins = e.tensor_copy(Qi[:, 0, :], Pi[:, 8, :])
                for k in range(5):
                    ins = cmul(e, Qr[:, k + 1, :], Qi[:, k + 1, :], Qr[:, k, :], Qi[:, k, :], Qr[:, k, :], Qi[:, k, :], 0)
                return ins
            dv(f_pow, ["a"], ["P", "Q"])

            Bbr = tb("Bbr", [128, 32, 16]); Bbi = tb("Bbi", [128, 32, 16])
            x1 = tb("x1", [128, 8, 32, 16]); x2 = tb("x2", [128, 8, 32, 16])
            bcg = lambda t: t[:].unsqueeze(2).to_broadcast([128, 32, 16])

            def f_bbar(e):
                e.tensor_tensor(out=x1[:, 0], in0=bre[:], in1=bcg(T["cr"]), op=ALU.mult)
                e.tensor_tensor(out=x2[:, 0], in0=bim[:], in1=bcg(T["ci"]), op=ALU.mult)
                e.tensor_tensor(out=Bbr[:], in0=x1[:, 0], in1=x2[:, 0], op=ALU.subtract)
                e.tensor_tensor(out=x1[:, 1], in0=bim[:], in1=bcg(T["cr"]), op=ALU.mult)
                e.tensor_tensor(out=x2[:, 1], in0=bre[:], in1=bcg(T["ci"]), op=ALU.mult)
                return e.tensor_tensor(out=Bbi[:], in0=x1[:, 1], in1=x2[:, 1], op=ALU.add)
            dv(f_bbar, ["zoh"] + BREK + BIMK, ["Bb"])

            BKs = tb("BKs", [128, 8, 32, 16])
            CIs = tb("CIs", [128, 8, 32, 16])
            CI0 = tb("CI0", [128, 32, 16])
            pk = lambda t, k0: t[:, k0:k0 + 8, :].unsqueeze(3).to_broadcast([128, 8, 32, 16])
            bk = lambda t: t[:].unsqueeze(1).to_broadcast([128, 8, 32, 16])

            def f_stack_c(e):
                lo, hi = slice(0, 64), slice(64, 128)
                e.tensor_copy(CI0[lo], cre[lo])
                e.tensor_scalar(out=CI0[hi], in0=cim[hi], scalar1=-1.0, scalar2=None, op0=ALU.mult)
                e.tensor_scalar(out=x1[lo, 0], in0=cim[lo], scalar1=-1.0, scalar2=None, op0=ALU.mult)
                e.tensor_scalar(out=cim[hi], in0=cre[hi], scalar1=-1.0, scalar2=None, op0=ALU.mult)
                return e.tensor_copy(cim[lo], x1[lo, 0])
            dv(f_stack_c, CREK + CIMK, ["Cs2", "CI0", "x"])

            def f_ci(e):
                e.tensor_tensor(out=x1[:], in0=bk(CI0), in1=pk(Pr, 1), op=ALU.mult)
                e.tensor_tensor(out=x2[:], in0=bk(cim), in1=pk(Pi, 1), op=ALU.mult)
                return e.tensor_tensor(out=CIs[:], in0=x1[:], in1=x2[:], op=ALU.add)
            dv(f_ci, ["x", "P", "Cs2", "CI0"], ["CIs", "x"])

            def f_stack_b(e):
                lo, hi = slice(0, 64), slice(64, 128)
                e.tensor_copy(x1[hi, 0], Bbr[hi])
                e.tensor_copy(Bbr[hi], Bbi[hi])
                e.tensor_copy(Bbi[hi], x1[hi, 0])
                pscr = x2[lo, 0].rearrange("p g h -> p (g h)")[:, 0:288].rearrange("p (k g) -> p k g", g=32)
                e.tensor_scalar(out=pscr, in0=Pi[lo], scalar1=-1.0, scalar2=None, op0=ALU.mult)
                return e.tensor_copy(Pi[lo], pscr)
            dv(f_stack_b, ["Bb", "P", "CIs", "x"], ["stk", "x", "Pneg"])

            def f_bk(e):
                e.tensor_tensor(out=x1[:], in0=bk(Bbr), in1=pk(Pr, 0), op=ALU.mult)
                e.tensor_tensor(out=x2[:], in0=bk(Bbi), in1=pk(Pi, 0), op=ALU.mult)
                return e.tensor_tensor(out=BKs[:], in0=x1[:], in1=x2[:], op=ALU.add)
            dv(f_bk, ["stk", "P", "Pneg"], ["BKs", "x"])

            def f_wi(e):
                for par in range(2):
                    cm = cst[:, (CO_CME, CO_CMO)[par]:(CO_CME, CO_CMO)[par] + 512].unsqueeze(1).to_broadcast([128, 8, 512])
                    ins = e.tensor_tensor(out=WI[:, par], in0=CIs[:].rearrange("p j g h -> p j (g h)"), in1=cm, op=ALU.mult)
                return ins
            dv(f_wi, ["CIs", "cst"], ["WI"])

            col1 = tb("col1", [128, 6, 32], BF16); col2 = tb("col2", [128, 6, 32], BF16)
            jmb = tb("jmb", [128, 64], BF16)

            def f_aa(e):
                e.tensor_copy(col1[0:64], Qr[0:64])
                e.tensor_scalar(out=col1[64:128], in0=Qi[64:128], scalar1=-1.0, scalar2=None, op0=ALU.mult)
                e.tensor_copy(col2[0:64], Qi[0:64])
                e.tensor_copy(col2[64:128], Qr[64:128])
                e.tensor_copy(jmb[:], jm)
                ins = None
                for k in range(6):
                    jb = jmb[:].unsqueeze(1).to_broadcast([128, 32, 64])
                    e.tensor_tensor(out=AAm[:, :, k, 0:64], in0=jb, in1=col1[:, k, :].unsqueeze(2).to_broadcast([128, 32, 64]), op=ALU.mult)
                    ins = e.tensor_tensor(out=AAm[:, :, k, 64:128], in0=jb, in1=col2[:, k, :].unsqueeze(2).to_broadcast([128, 32, 64]), op=ALU.mult)
                return ins
            dv(f_aa, ["Q", "cst"], ["AA"])

            A("dve", lambda e: e.tensor_scalar(out=hb[:], in0=hb[:], scalar1=0.5, scalar2=None, op0=ALU.mult), reads=["hb"], writes=["hb"])
            def gwf(e):
                for q in range(4):
                    ins = e.scalar_tensor_tensor(out=GW[:, q, :], in0=gwr[:, q].rearrange("p a b -> p (a b)"), scalar=1.0, in1=bd, op0=ALU.mult, op1=ALU.mult)
                return ins
            A("dve", gwf, reads=["gwr0", "gwr1", "gwr2", "gwr3", "cst"], writes=["GW"])

            tmpk = tb("tmpk", [128, 128])
            for q in range(4):
                for kh in range(2):
                    idx = q * 2 + kh
                    bm = B[(idx % 2) * 2]
                    bt = B[(idx % 2) * 2 + 1]
                    km, kt = "B%d" % ((idx % 2) * 2), "B%d" % ((idx % 2) * 2 + 1)
                    ci0 = CI0[:, q * 8:(q + 1) * 8, :].rearrange("p g h -> p (g h)")

                    def f_pe(e, bm=bm, bt=bt, q=q, kh=kh, ci0=ci0):
                        ins = None
                        for i in range(4):
                            k = kh * 4 + i
                            blk = BKs[:, k, q * 8:(q + 1) * 8, :].rearrange("p g h -> p (g h)")
                            e.matmul(bm[:, i * 128:(i + 1) * 128], lhsT=blk, rhs=ci0, start=True, stop=True)
                            ins = e.transpose(bt[:, i * 128:(i + 1) * 128], blk, ident)
                        return ins
                    A("pe", f_pe, reads=["BKs", "CI0", "cst"], writes=[km, kt])
                    bd4 = bd.unsqueeze(1).to_broadcast([128, 4, 128])
                    if kh == 0:
                        A("dve", lambda e, bm=bm: e.tensor_tensor(out=tmpk[:], in0=bm[:, 0:128], in1=bd, op=ALU.mult), reads=[km, "cst"], writes=["tmpk"])
                        A("dve", lambda e, bm=bm, q=q: e.tensor_tensor(out=WL[:, q, 1:4, :], in0=bm[:, 128:512].rearrange("p (i c) -> p i c", i=3), in1=bd.unsqueeze(1).to_broadcast([128, 3, 128]), op=ALU.mult),
                          reads=[km, "cst"], writes=[("WL", q, kh)])
                        A("dve", lambda e, q=q: e.scalar_tensor_tensor(out=WL[:, q, 0, :], in0=ident, scalar=dcol[:, q:q + 1], in1=tmpk[:], op0=ALU.mult, op1=ALU.add),
                          reads=["tmpk", "cst", "dcol"], writes=[("WL0", q)])
                    else:
                        A("dve", lambda e, bm=bm, q=q, kh=kh, bd4=bd4: e.tensor_tensor(out=WL[:, q, kh * 4:kh * 4 + 4, :], in0=bm[:].rearrange("p (i c) -> p i c", i=4), in1=bd4, op=ALU.mult),
                          reads=[km, "cst"], writes=[("WL", q, kh)])
                    for par in range(2):
                        A("act", lambda e, bt=bt, q=q, kh=kh, par=par: e.mul(out=WV[:, par, q, kh * 4:kh * 4 + 4, :], in_=bt[:].rearrange("p (i c) -> p i c", i=4),
                                                                           mul=cst[:, (CO_RME, CO_RMO)[par]:(CO_RME, CO_RMO)[par] + 1]),
                          reads=[kt, "cst"], writes=[("WV", q, kh, par)])

            if DBG.get("setup_dumps"):
                for nm, ap in [("step", T["step"][:]), ("lre", lre[:]), ("lim", lim[:]), ("lst", lst[:]), ("Ar", T["Ar"][:]), ("Ai", T["Ai"][:]), ("cr", T["cr"][:]), ("mag", T["mag"][:]), ("sn", T["sn"][:]), ("r", T["r"][:])]:
                    dd = nc.dram_tensor("sd_" + nm, [128, 32], F32, kind="ExternalOutput").ap()
                    A("sp", lambda e, dd=dd, ap=ap: e.dma_start(out=dd[:, :], in_=ap), reads=list(set(P.last_w)), dma_key="sd_" + nm)
                    DBG.setdefault("sd_keys", []).append("sd_" + nm)
                for nm, ap in [("cre", cre), ("cim", cim), ("bre", bre), ("bim", bim), ("Bbr", Bbr), ("CI0", CI0)]:
                    dd = nc.dram_tensor("sd_" + nm, [128, 512], F32, kind="ExternalOutput").ap()
                    A("sp", lambda e, dd=dd, ap=ap: e.dma_start(out=dd[:, :], in_=ap[:].rearrange("p g h -> p (g h)")), reads=list(set(P.last_w)), dma_key="sd_" + nm)
                    DBG.setdefault("sd_keys", []).append("sd_" + nm)
            A("pool", lambda e: e.memset(carry[0][:], 0.0), writes=["carry0"])
            A("pool", lambda e: e.memset(vT[:, :, 0:16], 0.0), writes=["halo"])

        P.barrier()

        xt = sb("xt", [128, 4, D])
        nbf = [sb("nbf0", [128, D], BF16)] * 2
        junk = nbf[0]
        nT = sb("nT", [128, 8, NT], BF16)
        pdum = sb("pdum", [128, 64])
        uT = sb("uT", [128, 4, NT], BF16)
        ss = sb("ss", [128, 16])
        rs = sb("rs", [128, 16])
        uni = sb("uni", [128, 5632])
        E32 = uni[:, 0:2048].rearrange("p (g c) -> p g c", c=64)
        Ebf = uni[:, 2048:3072].bitcast(BF16).rearrange("p (g c) -> p g c", c=64)
        sq = [uni[:, 3072:3584], uni[:, 3584:4096]]
        inn = [uni[:, 4096:4608], uni[:, 4608:5120]]
        th = [uni[:, 5120:5632], sb("th1", [128, NT])[:]]
        ffT = uni[:, :].bitcast(BF16).rearrange("p (j n) -> p j n", n=NT)
        ost = uni[:, 0:4096].rearrange("p (s d) -> p s d", d=D)
        UNIK = [("E32", b_) for b_ in range(4)] + [("Ebf", b_) for b_ in range(4)] + ["sq0", "sq1", "inn0", "inn1", "th0"] + [("ffT", j_) for j_ in range(NJ)]
        Sprev = sb("Sprev", [128, 32, 64], BF16)
        pa = sb("pa", [128, 16 + NT]); pb_ = sb("pb", [128, 16 + NT]); pc = sb("pc", [128, 16 + NT])
        pbf = [sb("pbf0", [128, NT], BF16)] * 2
        g2 = [sb("g2%d" % i, [128, NT]) for i in range(2)]
        g2b = [sb("g2b0", [128, NT], BF16)] * 2
        gbf = [g2b[0], sb("gbf1", [128, NT], BF16)]
        mixT = nT
        sqb = [sb("sqb0", [128, NT], BF16)] * 2
        sil = g2
        NR = 3
        NRD1 = 5
        NWO = 4
        NWI = 4
        wring = [sb("wr%d" % i, [128, 8, 128], BF16) for i in range(NWI)]
        woring = [sb("wo%d" % i, [128, 512], BF16) for i in range(NWO)]
        wgr = [sb("wg%d" % i, [128, 8, 128], BF16) for i in range(NR)]
        wur = [sb("wu%d" % i, [128, 8, 128], BF16) for i in range(NR)]
        wdr = [sb("wd%d" % i, [128, 512], BF16) for i in range(NR)]
        wd1r = [sb("wdd%d" % i, [128, 512], BF16) for i in range(NRD1)]
        ones = sb("ones", [128, 1], BF16)
        A("pool", lambda e: e.memset(ones[:], 1.0), writes=["ones"])
        A("pool", lambda e: e.memset(ss[:, 12:14], 1.0), writes=["warm"])

        cnt = {"win": 0, "wout": 0, "ffn": 0, "wd1": 0}
        win_issued = set()

        def emit_win(gi):
            win_issued.add(gi)
            m = gi % 8
            slot = gi % NWI
            A("sp", lambda e, slot=slot, m=m: e.dma_start(out=wring[slot][:], in_=win_t[m]), reads=["cv_win"], writes=["wr%d" % slot], dma_key="wr%d" % slot)


        def rmsnorm_to_T(npart, src_of, gcol, nsub, tag, eps):
            for s in range(nsub):
                A("act", lambda e, s=s: e.activation(out=junk[0:npart, :], in_=src_of(s), func=AF.Square, accum_out=ss[0:npart, s:s + 1]),
                  reads=[("xt", s)], writes=[("ss", s), "nbf0"])
            A("dve", lambda e: e.tensor_scalar(out=rs[0:npart, 0:nsub], in0=ss[0:npart, 0:nsub], scalar1=1.0 / D, scalar2=eps, op0=ALU.mult, op1=ALU.add), reads=[("ss", s_) for s_ in range(nsub)], writes=["rs"])
            A("act", lambda e: e.activation(out=ss[0:npart, 0:nsub], in_=rs[0:npart, 0:nsub], func=AF.Sqrt), reads=["rs"], writes=[("ss", s_) for s_ in range(nsub)])
            A("dve", lambda e: e.reciprocal(out=rs[0:npart, 0:nsub], in_=ss[0:npart, 0:nsub]), reads=[("ss", s_) for s_ in range(nsub)], writes=["rs"])
            nb2 = g2[0][:].bitcast(BF16)
            for s in range(nsub):
                nb, nbk = (nbf[0][:], "nbf0") if s % 2 == 0 else (nb2, "g20")
                A("act", lambda e, s=s, nb=nb: e.mul(out=nb[0:npart, :], in_=src_of(s), mul=rs[0:npart, s:s + 1]),
                  reads=["rs", ("xt", s)], writes=[nbk])
                for kb in range(2):
                    tp, tpk = (TP, "TP") if kb == 0 else (TP2, "B6")

                    def f_tr(e, s=s, kb=kb, nb=nb, tp=tp):
                        ins = None
                        for i in range(4):
                            k = kb * 4 + i
                            ins = e.transpose(tp[:, i * npart:(i + 1) * npart], nb[0:npart, k * 128:(k + 1) * 128], identb[0:npart, 0:npart])
                        return ins
                    A("pe", f_tr, reads=[nbk, "identb"], writes=[tpk])
                    A("dve", lambda e, s=s, kb=kb, tp=tp: e.tensor_tensor(out=nT[:, kb * 4:kb * 4 + 4, s * 128:s * 128 + npart], in0=tp[:, 0:4 * npart].rearrange("p (i n) -> p i n", i=4),
                                                                 in1=gcol[:, kb * 4:kb * 4 + 4].unsqueeze(2).to_broadcast([128, 4, npart]), op=ALU.mult),
                      reads=[tpk, tag], writes=[("nT", kb * 4 + i) for i in range(4)])

        def s5_scan(ntok, cur, nxt, has_carry, hooks=None):
            nch = ntok // 8
            steps = [s_ for s_ in (1, 2, 4, 8, 16, 32) if s_ < nch]
            def f_v(e):
                ins = None
                for q in range(4):
                    for par in range(2):
                        sl = q * 2 + par
                        for i in range(8):
                            for b in range(4):
                                u3 = uT[32 * b:32 * b + 32, q, 0:ntok].rearrange("p (c i) -> p c i", i=8)
                                ins = e.matmul(B[b][:, sl * 64:sl * 64 + nch], lhsT=WV[32 * b:32 * b + 32, par, q, 7 - i, :], rhs=u3[:, :, i],
                                               start=(i == 0), stop=(i == 7 and not has_carry), tile_position=(32 * b, 0))
                        if has_carry:
                            for b in range(4):
                                g = q * 8 + 2 * b + par
                                ins = e.matmul(B[b][:, sl * 64:sl * 64 + 1], lhsT=AAm[:, g, 0, :], rhs=cur[:, g:g + 1], start=False, stop=True)
                return ins
            A("pe", f_v, reads=[("uT", q_) for q_ in range(4)] + ["carry_cur"], writes=["B0", "B1", "B2", "B3"])
            if hooks and -1 in hooks:
                hooks[-1]()

            if DBG.get("scan_parts", 9) < 2 and ntok == NT:
                return

            def gview(t, b, lo, hi):
                return t[:, :, lo:hi].rearrange("p (q r) c -> p q r c", r=8)[:, :, 2 * b:2 * b + 2, :]

            def bview(b, lo, hi):
                return B[b][:].rearrange("p (q r c) -> p q r c", q=4, r=2)[:, :, :, lo:hi]
            for b in range(4):
                A("dve", lambda e, b=b: e.tensor_copy(gview(E32, b, 0, nch), bview(b, 0, nch)), reads=["B%d" % b], writes=[("E32", b)])
                A("act", lambda e, b=b: e.activation(out=gview(Ebf, b, 0, nch), in_=gview(E32, b, 0, nch), func=AF.Copy), reads=[("E32", b)], writes=[("Ebf", b)])
            if DBG.get("scan_parts", 9) < 3 and ntok == NT:
                return
            for si, sh in enumerate(steps):
                for b in range(4):
                    def f_l(e, b=b, si=si, sh=sh):
                        ins = None
                        for q in range(4):
                            for par in range(2):
                                g = q * 8 + 2 * b + par
                                sl = q * 2 + par
                                ins = e.matmul(B[b][:, sl * 64 + sh:sl * 64 + nch], lhsT=AAm[:, g, si, :], rhs=Ebf[:, g, 0:nch - sh], start=True, stop=True)
                        return ins
                    A("pe", f_l, reads=[("Ebf", b)], writes=["B%d" % b])
                    A("dve", lambda e, b=b, sh=sh: e.tensor_tensor(out=gview(E32, b, sh, nch), in0=gview(E32, b, sh, nch), in1=bview(b, sh, nch), op=ALU.add),
                      reads=["B%d" % b, ("E32", b)], writes=[("E32", b)])
                    A("act", lambda e, b=b, sh=sh: e.activation(out=gview(Ebf, b, sh, nch), in_=gview(E32, b, sh, nch), func=AF.Copy), reads=[("E32", b)], writes=[("Ebf", b)])
                if hooks and si in hooks:
                    hooks[si]()
            if DBG.get("scan_parts", 9) < 4 and ntok == NT:
                return
            if ntok == NT:
                A("pool", lambda e: e.tensor_copy(Sprev[:, :, 0:1], cur[:].unsqueeze(2)), reads=["carry_cur"], writes=["Sprev0"])
                A("dve", lambda e: e.tensor_copy(Sprev[:, :, 1:64], Ebf[:, :, 0:63]), reads=[("Ebf", b_) for b_ in range(4)], writes=["Sprev1"])
            A("dve", lambda e: e.tensor_copy(nxt[:].unsqueeze(2), E32[:, :, nch - 1:nch]), reads=[("E32", b_) for b_ in range(4)] + ["carry_cur"], writes=["carry_nxt"])

        def w_in_stage(ntok):
            base = cnt["win"]
            cnt["win"] += 8
            for m in range(8):
                slot = (base + m) % NWI
                wt = wring[slot]
                if (base + m) not in win_issued:
                    emit_win(base + m)
                bank = m % 4

                def f_mm(e, wt=wt, bank=bank):
                    ins = None
                    for k in range(8):
                        ins = e.matmul(B[bank][:, 0:ntok], lhsT=wt[:, k, :], rhs=nT[:, k, 0:ntok], start=(k == 0), stop=(k == 7))
                    return ins
                A("pe", f_mm, reads=["wr%d" % slot] + [("nT", k) for k in range(8)], writes=["B%d" % bank])
                if m + NWI < 8:
                    emit_win(base + m + NWI)
                if m < 4:
                    A("act", lambda e, m=m, bank=bank: e.activation(out=uT[:, m, 0:ntok], in_=B[bank][:, 0:ntok], func=AF.Copy), reads=["B%d" % bank], writes=[("uT", m)])
                else:
                    off = 16 if ntok == NT else 0
                    A("dve", lambda e, m=m, bank=bank, off=off: e.tensor_copy(vT[:, m - 4, off:off + ntok], B[bank][:, 0:ntok]), reads=["B%d" % bank], writes=[("vT", m - 4), "halo"])

        if LV >= 1:
            A("sp", lambda e: e.dma_start(out=xt[0:16, 0, :], in_=meta[:, :]), writes=[("xt", 0)], dma_key="xt0")
            rmsnorm_to_T(16, lambda s: xt[0:16, 0, :], g1col, 1, "g1col", EPS)
        if LV >= 2:
            w_in_stage(16)
        if LV >= 3:
            s5_scan(16, carry[0], carry[1], False)

        wo_order = [(br, h, c4) for br in range(2) for h in range(2) for c4 in range(4)]

        def emit_wo(base, i_):
            br, h, c4 = wo_order[i_]
            c = br * 4 + c4
            slot = (base + i_) % NWO
            wt = woring[slot]
            src = wout_t[i_]
            A("sp", lambda e, wt=wt, src=src: e.dma_start(out=wt[:, 0:512], in_=src), reads=["cv_wout"], writes=["wo%d" % slot], dma_key="wo%d" % slot)

        def emit_ffn(base, j):
            slot = (base + j) % NR
            A("sp", lambda e, slot=slot, j=j: e.dma_start(out=wgr[slot][:], in_=wg_t[j]), reads=["cv_wg"], writes=["wg%d" % slot], dma_key="wg%d" % slot)
            A("sp", lambda e, slot=slot, j=j: e.dma_start(out=wur[slot][:], in_=wu_t[j]), reads=["cv_wu"], writes=["wu%d" % slot], dma_key="wu%d" % slot)
            A("sp", lambda e, slot=slot, j=j: e.dma_start(out=wdr[slot][:], in_=wd0_t[j]), reads=["cv_wd0"], writes=["wd%d" % slot], dma_key="wd%d" % slot)

        def emit_wd1(base, j):
            slot = (base + j) % NRD1
            A("sp", lambda e, slot=slot, j=j: e.dma_start(out=wd1r[slot][:], in_=wd1_t[j]), reads=["cv_wd1"], writes=["wdd%d" % slot], dma_key="wdd%d" % slot)

        def emit_xload(tt):
            xsrc = x[tt * NT:(tt + 1) * NT, :].rearrange("(s p) d -> p s d", p=128)
            for s_ in range(4):
                A("sp", lambda e, xsrc=xsrc, s_=s_: e.dma_start(out=xt[:, s_, :], in_=xsrc[:, s_, :]), writes=[("xt", s_)], dma_key="xt%d" % s_)

        for t in range(DBG["ntiles"] if LV >= 4 else 0):
            cur, nxt = carry[(t + 1) % 2], carry[t % 2]
            P.last_w["carry_cur"] = P.last_w.get("carry_nxt")
            P.readers["carry_cur"] = []
            if t == 0:
                emit_xload(0)
            rmsnorm_to_T(128, lambda s: xt[:, s, :], g1col, 4, "g1col", EPS)
            if LV < 4.3:
                continue
            def pool_k(k, part):
                w = 2 ** (k + 1)
                v = vT[:, k, :]
                bufs = [pa, pb_, pc]

                def f_pool(e, k=k, v=v):
                    e0 = e
                    e = Serial(e0, P.chain["pool"])
                    src = v
                    ins = None
                    lo = 16 - (2 ** (k + 1) - 1)
                    sh = 1
                    lvl = 0
                    while sh < 2 ** (k + 1):
                        lo_l = lo + (2 * sh - 1)
                        dst = bufs[lvl % 3][:]
                        ins = e.tensor_tensor(out=dst[:, lo_l:16 + NT], in0=src[:, lo_l:16 + NT], in1=src[:, lo_l - sh:16 + NT - sh], op=ALU.add)
                        src = dst
                        sh *= 2
                        lvl += 1
                    return e0.nop()
                lastbuf = bufs[k % 3]
                i2 = k % 2
                yb = 4 + (k % 2)
                ykey = "B%d" % yb
                if part == "stats":
                    def f_ss2(e, k=k, i2=i2):
                        ins = None
                        for s in range(4):
                            ins = e.matmul(ST[:, 2 * s + 1:2 * s + 2], lhsT=sqb[i2][:, s * 128:(s + 1) * 128], rhs=ones[:, 0:1], start=(k == 0 and s == 0), stop=(k == 3), skip_group_check=True)
                        return ins
                    A("pe", f_ss2, reads=["sqb0", "ones"], writes=["STb"])
                    return
                if part == "mm":
                    A("pe", lambda e, k=k, yb=yb, i2=i2: e.matmul(B[yb][:], lhsT=PW[:, k, :], rhs=pbf[i2][:], start=True, stop=True), reads=["pbf0", "PW"], writes=[ykey])
                    A("dve", lambda e, k=k, yb=yb, w=w: e.tensor_scalar(out=mixT[:, 4 + k, :], in0=B[yb][:], scalar1=pscale[:, k:k + 1], scalar2=1.0 / w, op0=ALU.mult, op1=ALU.mult), reads=[ykey], writes=[("nT", 4 + k)])
                    A("act", lambda e, k=k, i2=i2: e.activation(out=sqb[i2][:], in_=mixT[:, 4 + k, :], func=AF.Square), reads=[("nT", 4 + k)], writes=["sqb0"])
                    A("dve", lambda e, k=k: e.tensor_scalar(out=mixT[:, 4 + k, :], in0=mixT[:, 4 + k, :], scalar1=gocol[:, 4 + k:5 + k], scalar2=None, op0=ALU.mult), reads=[("nT", 4 + k), "gocol"], writes=[("nT", 4 + k)])
                    return
                A("pool", f_pool, reads=[("vT", k), "halo"], writes=["pwin"])
                A("dve", lambda e, k=k, v=v, lastbuf=lastbuf, i2=i2, w=w: e.scalar_tensor_tensor(out=pbf[i2][:], in0=v[:, 16:16 + NT], scalar=-float(w), in1=lastbuf[:, 16:16 + NT], op0=ALU.mult, op1=ALU.add),
                  reads=["pwin", ("vT", k)], writes=["pbf0"])
                A("pool", lambda e, k=k: e.tensor_copy(vT[:, k, 0:16], vT[:, k, NT:NT + 16]), reads=[("vT", k)], writes=["halo"])

            w_in_stage(NT)
            A("act", lambda e: e.activation(out=ss[:, 13:14], in_=ss[:, 12:13], func=AF.Gelu_apprx_tanh), reads=["warm"], writes=["warm2"])
            if LV < 4.6:
                continue
            s5_scan(NT, cur, nxt, True, hooks={-1: lambda: pool_k(0, "front"),
                                                  0: lambda: (pool_k(0, "mm"), pool_k(1, "front")),
                                                  1: lambda: (pool_k(0, "stats"), pool_k(1, "mm"), pool_k(2, "front")),
                                                  2: lambda: (pool_k(1, "stats"), pool_k(2, "mm"), pool_k(3, "front")),
                                                  3: lambda: (pool_k(2, "stats"), pool_k(3, "mm")),
                                                  4: lambda: pool_k(3, "stats")} if LV >= 6 else None)

            if LV < 5:
                continue
            for q in range(4):
                yb = q
                ykey = "B%d" % yb

                def f_y(e, q=q, yb=yb):
                    o3 = B[yb][:].rearrange("p (c j) -> p c j", j=8)
                    u3 = uT[:, q, :].rearrange("p (c j) -> p c j", j=8)
                    e.matmul(B[yb][:], lhsT=WL[:, q, 0, :], rhs=uT[:, q, :], start=True, stop=False, skip_group_check=True)
                    for k in range(1, 8):
                        e.matmul(o3[:, :, k:8], lhsT=WL[:, q, k, :], rhs=u3[:, :, 0:8 - k], start=False, stop=False, skip_group_check=True)
                    ins = None
                    for par in range(2):
                        for j in range(8):
                            for blk in range(4):
                                o3b = B[yb][32 * blk:32 * blk + 32, :].rearrange("p (c j) -> p c j", j=8)
                                g = q * 8 + blk * 2 + par
                                pair = q * 4 + blk
                                ins = e.matmul(o3b[:, :, j], lhsT=WI[:, par, j, pair * 32:pair * 32 + 32], rhs=Sprev[:, g, :],
                                               start=False, stop=(par == 1 and j == 7), tile_position=(0, 32 * blk), skip_group_check=True)
                    return ins
                A("pe", f_y, reads=[("uT", q), "Sprev0", "Sprev1"], writes=[ykey])
            if LV < 6:
                continue
            if LV < 6.5:
                continue
            wo_base = t * 16
            ffn_base = t * NJ
            wd1_base = t * NJ
            for i_ in range(NWO):
                emit_wo(wo_base, i_)
            def emit_s5_stats(q):
                def f_ss(e, q=q):
                    ins = None
                    for s in range(4):
                        ins = e.matmul(ST[:, 2 * s:2 * s + 1], lhsT=sqb[0][:, s * 128:(s + 1) * 128], rhs=ones[:, 0:1], start=False, stop=(q == 3), skip_group_check=True)
                    return ins
                A("pe", f_ss, reads=["sqb0", "ones"], writes=["STa"])
            def emit_gelu(q):
                A("act", lambda e, q=q: e.activation(out=gbf[q % 2][:], in_=B[q][:], func=AF.Gelu_apprx_tanh), reads=["B%d" % q], writes=["gbf%d" % (q % 2)])
            emit_gelu(0)
            emit_gelu(1)
            for q in range(4):
                i2 = q % 2
                gbank = 6 if q % 2 == 0 else 5
                gk = "B%d" % gbank
                A("pe", lambda e, q=q, i2=i2, gbank=gbank: e.matmul(B[gbank][:], lhsT=GW[:, q, :], rhs=gbf[i2][:], start=True, stop=True), reads=["gbf%d" % i2], writes=[gk])
                if q > 0:
                    emit_s5_stats(q - 1)
                A("act", lambda e, q=q, i2=i2, gbank=gbank: e.activation(out=th[i2][:], in_=B[gbank][:], func=AF.Tanh, scale=0.5, bias=hb[:, q:q + 1]), reads=[gk], writes=["th%d" % i2])
                A("dve", lambda e, q=q, i2=i2: e.scalar_tensor_tensor(out=mixT[:, q, :], in0=th[i2][:], scalar=1.0, in1=gbf[i2][:], op0=ALU.add, op1=ALU.mult), reads=["th%d" % i2, "gbf%d" % i2], writes=[("nT", q)])
                if q + 2 < 4:
                    emit_gelu(q + 2)
                A("act", lambda e, q=q: e.activation(out=sqb[0][:], in_=mixT[:, q, :], func=AF.Square), reads=[("nT", q)], writes=["sqb0"])
                A("dve", lambda e, q=q: e.tensor_scalar(out=mixT[:, q, :], in0=mixT[:, q, :], scalar1=gocol[:, q:q + 1], scalar2=None, op0=ALU.mult), reads=[("nT", q), "gocol"], writes=[("nT", q)])
            emit_s5_stats(3)
            if LV < 7:
                continue
            def f_rs(e):
                st3 = ST[:, 0:8].rearrange("p (s b) -> p s b", b=2)
                r3 = rs[:, 0:8].rearrange("p (s b) -> p s b", b=2)
                e.tensor_scalar(out=r3[:, :, 0], in0=st3[:, :, 0], scalar1=1.0 / 512, scalar2=4.0 * EPS, op0=ALU.mult, op1=ALU.add)
                return e.tensor_scalar(out=r3[:, :, 1], in0=st3[:, :, 1], scalar1=1.0 / 512, scalar2=EPS, op0=ALU.mult, op1=ALU.add)
            A("dve", f_rs, reads=["STa", "STb"], writes=["rs"])
            A("act", lambda e: e.activation(out=ss[:, 0:8], in_=rs[:, 0:8], func=AF.Sqrt), reads=["rs"], writes=[("ss", s_) for s_ in range(4)] + ["ss8"])
            A("dve", lambda e: e.reciprocal(out=rs[:, 0:8], in_=ss[:, 0:8]), reads=[("ss", s_) for s_ in range(4)] + ["ss8"], writes=["rs"])

            wo_list = [(br, h, c4) for br in range(2) for h in range(2) for c4 in range(4)]
            for i_, (br, h, c4) in enumerate(wo_list):
                c = br * 4 + c4
                slot = (wo_base + i_) % NWO
                wt = woring[slot]

                for s in range(4):
                    A("pe", lambda e, wt=wt, c=c, c4=c4, s=s: e.matmul(B[s][:], lhsT=mixT[:, c, s * 128:(s + 1) * 128], rhs=wt[:, 0:512], start=(c4 == 0), stop=(c4 == 3)),
                      reads=["wo%d" % slot, ("nT", c)], writes=["B%d" % s])
                if i_ + NWO < len(wo_list):
                    emit_wo(wo_base, i_ + NWO)
                if c4 == 3:
                    for s in range(4):
                        A("dve", lambda e, s=s, h=h, br=br: e.scalar_tensor_tensor(out=xt[:, s, h * 512:(h + 1) * 512], in0=B[s][:], scalar=rs[:, 2 * s + br:2 * s + br + 1], in1=xt[:, s, h * 512:(h + 1) * 512], op0=ALU.mult, op1=ALU.add),
                          reads=["B%d" % s, "rs", ("xt", s)], writes=[("xt", s)])
            if LV < 8:
                continue
            for j in range(NR):
                emit_ffn(ffn_base, j)
            for j in range(NRD1):
                emit_wd1(wd1_base, j)
            if t + 1 < DBG["ntiles"]:
                for m in range(NWI):
                    emit_win(cnt["win"] + m)
            rmsnorm_to_T(128, lambda s: xt[:, s, :], g2col, 4, "g2col", EPS)
            pend = []

            def emit_down(slot, j):
                def f_d(e, slot=slot, j=j):
                    ins = None
                    for s in range(4):
                        ins = e.matmul(B[s][:], lhsT=ffT[:, j, s * 128:(s + 1) * 128], rhs=wdr[slot][:], start=(j == 0), stop=(j == NJ - 1))
                    return ins
                A("pe", f_d, reads=["wd%d" % slot, ("ffT", j)], writes=["B0", "B1", "B2", "B3"])
                if j + NR < NJ:
                    emit_ffn(ffn_base, j + NR)
            for j in range(NJ):
                slot = (ffn_base + j) % NR
                gb = 4 if j % 2 == 0 else 6
                gkey = "B%d" % gb

                def f_g(e, slot=slot, gb=gb):
                    ins = None
                    for k in range(8):
                        ins = e.matmul(B[gb][:], lhsT=wgr[slot][:, k, :], rhs=nT[:, k, :], start=(k == 0), stop=(k == 7))
                    return ins
                A("pe", f_g, reads=["wg%d" % slot] + [("nT", k) for k in range(8)], writes=[gkey])

                def f_u(e, slot=slot):
                    ins = None
                    for k in range(8):
                        ins = e.matmul(B[5][:], lhsT=wur[slot][:, k, :], rhs=nT[:, k, :], start=(k == 0), stop=(k == 7))
                    return ins
                A("pe", f_u, reads=["wu%d" % slot] + [("nT", k) for k in range(8)], writes=["B5"])
                i2 = j % 2
                A("act", lambda e, gb=gb, i2=i2: e.activation(out=sil[i2][:], in_=B[gb][:], func=AF.Silu), reads=[gkey], writes=["g2%d" % i2])
                A("dve", lambda e, j=j, i2=i2: e.tensor_tensor(out=ffT[:, j, :], in0=sil[i2][:], in1=B[5][:], op=ALU.mult), reads=["g2%d" % i2, "B5"], writes=[("ffT", j)])

                pend.append((slot, j))
                if len(pend) > 1:
                    emit_down(*pend.pop(0))
            while pend:
                emit_down(*pend.pop(0))
            for s in range(4):
                A("dve", lambda e, s=s: e.tensor_tensor(out=xt[:, s, 0:512], in0=xt[:, s, 0:512], in1=B[s][:], op=ALU.add), reads=["B%d" % s, ("xt", s)], writes=[("xt", s)])
            A("act", lambda e: e.activation(out=ss[:, 13:14], in_=ss[:, 12:13], func=AF.Sqrt), reads=["warm"], writes=["warm2"])
            for j in range(NJ):
                slot = (wd1_base + j) % NRD1

                def f_d1(e, slot=slot, j=j):
                    ins = None
                    for s in range(4):
                        ins = e.matmul(B[s][:], lhsT=ffT[:, j, s * 128:(s + 1) * 128], rhs=wd1r[slot][:], start=(j == 0), stop=(j == NJ - 1))
                    return ins
                A("pe", f_d1, reads=["wdd%d" % slot, ("ffT", j)], writes=["B0", "B1", "B2", "B3"])
                if j + NRD1 < NJ:
                    emit_wd1(wd1_base, j + NRD1)
            for s in range(4):
                A("dve", lambda e, s=s: e.tensor_tensor(out=xt[:, s, 512:1024], in0=xt[:, s, 512:1024], in1=B[s][:], op=ALU.add), reads=["B%d" % s, ("xt", s)], writes=[("xt", s)])
            if LV < 9:
                continue
            for s in range(4):
                A("act", lambda e, s=s: e.activation(out=junk[:], in_=xt[:, s, :], func=AF.Square, accum_out=ss[:, s:s + 1]), reads=[("xt", s)], writes=[("ss", s), "nbf0"])
            A("dve", lambda e: e.tensor_scalar(out=rs[:, 0:4], in0=ss[:, 0:4], scalar1=1.0 / D, scalar2=EPS, op0=ALU.mult, op1=ALU.add), reads=[("ss", s_) for s_ in range(4)], writes=["rs"])
            A("act", lambda e: e.activation(out=ss[:, 0:4], in_=rs[:, 0:4], func=AF.Sqrt), reads=["rs"], writes=[("ss", s_) for s_ in range(4)])
            A("dve", lambda e: e.reciprocal(out=rs[:, 0:4], in_=ss[:, 0:4]), reads=[("ss", s_) for s_ in range(4)], writes=["rs"])
            for s in range(4):
                A("dve", lambda e, s=s: e.scalar_tensor_tensor(out=ost[:, s, :], in0=xt[:, s, :], scalar=rs[:, s:s + 1], in1=fgb[:], op0=ALU.mult, op1=ALU.mult), reads=["rs", ("xt", s), "fgb"], writes=UNIK)
            if t + 1 < DBG["ntiles"]:
                emit_xload(t + 1)
            ydst = y[t * NT:(t + 1) * NT, :].rearrange("(s p) d -> p s d", p=128)
            A("sp", lambda e, ydst=ydst: e.dma_start(out=ydst, in_=ost), reads=UNIK, dma_key="yout")

        dumpable = {
            "WV": WV[:].rearrange("p a q k c -> p (a q k c)"), "WI": WI[:].rearrange("p a j c -> p (a j c)"),
            "WL": WL[:].rearrange("p q k c -> p (q k c)"), "AA": AAm[:, 0:4].rearrange("p g k c -> p (g k c)"),
            "GW": GW[:].rearrange("p q c -> p (q c)"), "hb": hb[:], "carry0": carry[0][:], "carry1": carry[1][:],
            "nT": nT[:].rearrange("p k n -> p (k n)"), "uT": uT[:].rearrange("p k n -> p (k n)"),
            "vT": vT[:].rearrange("p k n -> p (k n)"), "E32": E32.rearrange("p g c -> p (g c)"),
            "Sprev": Sprev[:].rearrange("p g c -> p (g c)"), "xt": xt[:].rearrange("p s d -> p (s d)"), "rs": rs[:],
        }
        fw = []
        for i, name in enumerate(DBG["dumps"]):
            ap = dumpable[name]
            dd = nc.dram_tensor("dump_" + name, [128, ap.shape[1]], F32, kind="ExternalOutput").ap()
            allk = list(set(P.last_w))
            A("pool", lambda e, dd=dd, ap=ap: e.dma_start(out=dd[:, :], in_=ap), reads=allk, dma_key="dump%d" % i)
            fw.append("dump%d" % i)
        fw += DBG.get("sd_keys", [])
        if any(o["dma_key"] == "yout" for o in P.ops):
            fw.append("yout")
        print('sbuf bytes remaining', nc.sbuf_bytes_remaining)
        P.emit(nc, final_wait_keys=fw)
    return nc


_NC_CACHE = {}


def kernel(**inputs):
    if "nc" not in _NC_CACHE:
        _NC_CACHE["nc"] = build_nc()
    nc = _NC_CACHE["nc"]
    f = lambda a: np.ascontiguousarray(np.asarray(a, dtype=np.float32))
    shared = {
        "meta_tokens": f(inputs["meta_tokens"]),
        "cst": make_consts(),
        "norm1_g": f(inputs["norm1_g"]).reshape(D),
        "w_in": f(inputs["w_in"]).reshape(D, D),
        "ssm_lambda_re": f(inputs["ssm_lambda_re"]).reshape(32, 64),
        "ssm_lambda_im": f(inputs["ssm_lambda_im"]).reshape(32, 64),
        "ssm_log_step": f(inputs["ssm_log_step"]).reshape(32),
        "ssm_b_re": f(inputs["ssm_b_re"]).reshape(32, 64, 16),
        "ssm_b_im": f(inputs["ssm_b_im"]).reshape(32, 64, 16),
        "ssm_c_re": f(inputs["ssm_c_re"]).reshape(32, 16, 64),
        "ssm_c_im": f(inputs["ssm_c_im"]).reshape(32, 16, 64),
        "ssm_d": f(inputs["ssm_d"]).reshape(512),
        "ssm_glu_w": f(inputs["ssm_glu_w"]).reshape(32, 16, 16),
        "ssm_glu_b": f(inputs["ssm_glu_b"]).reshape(512),
        "ssm_norm_g": f(inputs["ssm_norm_g"]).reshape(512),
        "pool_w": f(inputs["pool_w"]).reshape(4, 128, 128),
        "pool_scale": f(inputs["pool_scale"]).reshape(512),
        "pool_norm_g": f(inputs["pool_norm_g"]).reshape(512),
        "w_out": f(inputs["w_out"]).reshape(D, D),
        "norm2_g": f(inputs["norm2_g"]).reshape(D),
        "w_gate": f(inputs["w_gate"]).reshape(D, DFF),
        "w_up": f(inputs["w_up"]).reshape(D, DFF),
        "w_down": f(inputs["w_down"]).reshape(DFF, D),
        "final_norm_g": f(inputs["final_norm_g"]).reshape(D),
    }
    xs = f(inputs["x"])
    in_maps = [dict(shared, x=xs[b]) for b in range(8)]
    res = run_bass_kernel_spmd(nc, in_maps, core_ids=list(range(8)))
    return np.stack([r["y"] for r in res.results], axis=0).astype(np.float32)
```
